# Optimizing a Trainium2 kernel written in Bass

```python
import math
import jax, jax.numpy as jnp
from jax import lax
import numpy as np

D_MODEL = 1024
BATCH = 4
SEQ = 8192
DEPTH = 1

MIX_WIDTH = D_MODEL
GLA_WIDTH = MIX_WIDTH // 2
GMLP_WIDTH = MIX_WIDTH - GLA_WIDTH
GLA_HEADS = 4
GLA_DV = GLA_WIDTH // GLA_HEADS
GLA_DK = GLA_DV // 2
GLA_KEY_WIDTH = GLA_HEADS * GLA_DK
GLA_LOWRANK = 16
GLA_TAU = 16.0
GLA_CHUNK = 64
GMLP_GROUPS = 4
GMLP_GROUP_DIM = GMLP_WIDTH // GMLP_GROUPS
GMLP_CHUNK = 128
D_FF = int(math.ceil(8 * D_MODEL / 3 / 256) * 256)
EPS = 1e-6

PROJ_SIZES = [GLA_KEY_WIDTH, GLA_KEY_WIDTH, GLA_WIDTH, GLA_WIDTH,
              GLA_LOWRANK, GLA_LOWRANK, 2 * GMLP_WIDTH]
PROJ_WIDTH = sum(PROJ_SIZES)
PROJ_SPLITS = [int(v) for v in np.cumsum(PROJ_SIZES)[:-1]]

kernel_name = "hybrid_gla_gmlp_encoder_block"


def rmsnorm(x, g):
    xf = x.astype(jnp.float32)
    y = xf * lax.rsqrt(jnp.mean(xf * xf, axis=-1, keepdims=True) + EPS)
    return (y * g.astype(jnp.float32)).astype(x.dtype)


def layernorm(x, g, b):
    xf = x.astype(jnp.float32)
    mu = jnp.mean(xf, axis=-1, keepdims=True)
    xc = xf - mu
    y = xc * lax.rsqrt(jnp.mean(xc * xc, axis=-1, keepdims=True) + EPS)
    return (y * g.astype(jnp.float32) + b.astype(jnp.float32)).astype(x.dtype)


def gla_one_direction(q, k, v, log_a):
    B, S, H, DK = q.shape
    DV = v.shape[-1]
    C = GLA_CHUNK
    N = S // C
    f32 = jnp.float32
    q = q.astype(f32).reshape(B, N, C, H, DK)
    k = k.astype(f32).reshape(B, N, C, H, DK)
    v = v.astype(f32).reshape(B, N, C, H, DV)
    b = jnp.cumsum(log_a.astype(f32).reshape(B, N, C, H, DK), axis=2)
    b_last = b[:, :, -1]
    q_dec = q * jnp.exp(b)
    k_dec = k * jnp.exp(-b)
    k_to_end = k * jnp.exp(b_last[:, :, None] - b)
    scores = jnp.einsum('bnthd,bnshd->bnhts', q_dec, k_dec)
    tril = jnp.tril(jnp.ones((C, C), dtype=bool))
    scores = jnp.where(tril, scores, 0.0)
    o_intra = jnp.einsum('bnhts,bnshv->bnthv', scores, v)
    d_state = jnp.einsum('bnshd,bnshv->bnhdv', k_to_end, v)
    chunk_decay = jnp.exp(b_last)

    def step(state, inp):
        ds, dec = inp
        return dec[..., None] * state + ds, state

    state0 = jnp.zeros((B, H, DK, DV), f32)
    _, states_before = lax.scan(step, state0,
                                (jnp.moveaxis(d_state, 1, 0), jnp.moveaxis(chunk_decay, 1, 0)))
    states_before = jnp.moveaxis(states_before, 0, 1)
    o_inter = jnp.einsum('bnthd,bnhdv->bnthv', q_dec, states_before)
    return (o_intra + o_inter).reshape(B, S, H, DV)


def gla_mixer(h_q, h_k, h_v, h_g, lr_f, lr_b, w_decay_f, b_decay_f, w_decay_b, b_decay_b, gla_norm_g):
    B, S, _ = h_q.shape
    f32 = jnp.float32
    q = h_q.reshape(B, S, GLA_HEADS, GLA_DK) * (GLA_DK ** -0.5)
    k = h_k.reshape(B, S, GLA_HEADS, GLA_DK)
    v = h_v.reshape(B, S, GLA_HEADS, GLA_DV)
    la_f = (jax.nn.log_sigmoid((lr_f @ w_decay_f + b_decay_f).astype(f32)) / GLA_TAU
            ).reshape(B, S, GLA_HEADS, GLA_DK)
    la_b = (jax.nn.log_sigmoid((lr_b @ w_decay_b + b_decay_b).astype(f32)) / GLA_TAU
            ).reshape(B, S, GLA_HEADS, GLA_DK)
    o_fwd = gla_one_direction(q, k, v, la_f)
    o_bwd = jnp.flip(gla_one_direction(jnp.flip(q, 1), jnp.flip(k, 1), jnp.flip(v, 1),
                                       jnp.flip(la_b, 1)), 1)
    o = o_fwd + o_bwd
    o = o * lax.rsqrt(jnp.mean(o * o, axis=-1, keepdims=True) + EPS)
    o = o.reshape(B, S, GLA_WIDTH) * gla_norm_g.astype(f32)
    return (o * jax.nn.silu(h_g.astype(f32))).astype(h_q.dtype)


def gmlp_mixer(h_uv, ln_g, ln_b, w_spatial, b_spatial):
    B, S, _ = h_uv.shape
    z = jax.nn.gelu(h_uv, approximate=False)
    u, v = jnp.split(z, 2, axis=-1)
    v = layernorm(v, ln_g, ln_b)
    v = v.reshape(B, S // GMLP_CHUNK, GMLP_CHUNK, GMLP_GROUPS, GMLP_GROUP_DIM)
    s = jnp.einsum('gij,bnjgc->bnigc', w_spatial, v) + b_spatial.T[None, None, :, :, None]
    return u * s.reshape(B, S, GMLP_WIDTH)


def setup_inputs(seed: int = 0) -> dict:
    key = jax.random.key(seed)
    ks = jax.random.split(key, 20)
    L = DEPTH
    nrm = lambda k, shape, fan_in: jax.random.normal(k, shape, jnp.float32) * (fan_in ** -0.5)
    gain = lambda k, shape: 1.0 + 0.02 * jax.random.normal(k, shape, jnp.float32)
    small = lambda k, shape: 0.01 * jax.random.normal(k, shape, jnp.float32)
    return {
        "x": jax.random.normal(ks[0], (BATCH, SEQ, D_MODEL), jnp.float32),
        "norm1_g": gain(ks[1], (L, D_MODEL)),
        "w_in": nrm(ks[2], (L, D_MODEL, PROJ_WIDTH), D_MODEL),
        "w_decay_f": nrm(ks[3], (L, GLA_LOWRANK, GLA_KEY_WIDTH), GLA_LOWRANK),
        "b_decay_f": small(ks[4], (L, GLA_KEY_WIDTH)),
        "w_decay_b": nrm(ks[5], (L, GLA_LOWRANK, GLA_KEY_WIDTH), GLA_LOWRANK),
        "b_decay_b": small(ks[6], (L, GLA_KEY_WIDTH)),
        "gla_norm_g": gain(ks[7], (L, GLA_WIDTH)),
        "gmlp_ln_g": gain(ks[8], (L, GMLP_WIDTH)),
        "gmlp_ln_b": small(ks[9], (L, GMLP_WIDTH)),
        "w_spatial": nrm(ks[10], (L, GMLP_GROUPS, GMLP_CHUNK, GMLP_CHUNK), GMLP_CHUNK),
        "b_spatial": gain(ks[11], (L, GMLP_GROUPS, GMLP_CHUNK)),
        "w_out": nrm(ks[12], (L, MIX_WIDTH, D_MODEL), MIX_WIDTH),
        "norm2_g": gain(ks[13], (L, D_MODEL)),
        "w_gate": nrm(ks[14], (L, D_MODEL, D_FF), D_MODEL),
        "w_up": nrm(ks[15], (L, D_MODEL, D_FF), D_MODEL),
        "w_down": nrm(ks[16], (L, D_FF, D_MODEL), D_FF),
        "final_norm_g": gain(ks[17], (D_MODEL,)),
    }


def reference(x, norm1_g, w_in, w_decay_f, b_decay_f, w_decay_b, b_decay_b, gla_norm_g,
              gmlp_ln_g, gmlp_ln_b, w_spatial, b_spatial, w_out, norm2_g, w_gate, w_up,
              w_down, final_norm_g):
    for l in range(DEPTH):
        h = rmsnorm(x, norm1_g[l])
        p = h @ w_in[l]
        h_q, h_k, h_v, h_g, lr_f, lr_b, h_uv = jnp.split(p, PROJ_SPLITS, axis=-1)
        y_a = gla_mixer(h_q, h_k, h_v, h_g, lr_f, lr_b, w_decay_f[l], b_decay_f[l],
                        w_decay_b[l], b_decay_b[l], gla_norm_g[l])
        y_b = gmlp_mixer(h_uv, gmlp_ln_g[l], gmlp_ln_b[l], w_spatial[l], b_spatial[l])
        x = x + jnp.concatenate([y_a, y_b.astype(y_a.dtype)], axis=-1) @ w_out[l]
        h2 = rmsnorm(x, norm2_g[l])
        x = x + (jax.nn.silu(h2 @ w_gate[l]) * (h2 @ w_up[l])) @ w_down[l]
    return rmsnorm(x, final_norm_g)
```

```python
import numpy as np
from contextlib import ExitStack
import concourse.bass as bass
import concourse.mybir as mybir
from concourse.bass_utils import run_bass_kernel_spmd

F32 = mybir.dt.float32
BF16 = mybir.dt.bfloat16
AF = mybir.ActivationFunctionType
ALU = mybir.AluOpType

D = 1024
DFF = 2816
NFM = 1536
NTM = 1312
NIN = NFM + NTM
EPS = 1e-6
C_ID, C_ONE, C_UN, C_UTN, C_LSN, C_USN, C_MF, C_MB = 0, 128, 256, 384, 512, 640, 768, 1280
NCONST = 1792


class _Op:
    __slots__ = ("eng", "fn", "deps", "ddeps", "signal", "val", "dsem")


class Sched:
    ENG = ("pe", "act", "dve", "pool", "sp")

    def __init__(self):
        self.ops = {e: [] for e in self.ENG}
        self.tok = {}
        self.dsems = {}

    PSUM_TOK = ("PT", "PA", "PB", "PC", "PD", "PE", "PF", "PG")

    def add(self, eng, fn, reads=(), writes=(), dsem=None):
        pr = [t for t in reads if t in self.PSUM_TOK]
        if pr:
            reads = [t for t in reads if t not in self.PSUM_TOK]
            writes = list(writes) + pr
        op = _Op()
        op.eng, op.fn, op.signal, op.val, op.dsem = eng, fn, False, 0, dsem
        deps = set()
        for t in reads:
            st = self.tok.setdefault(t, [None, []])
            if st[0] is not None:
                deps.add(st[0])
        for t in writes:
            st = self.tok.setdefault(t, [None, []])
            if st[0] is not None:
                deps.add(st[0])
            deps.update(st[1])
        for t in reads:
            self.tok[t][1].append(op)
        for t in writes:
            self.tok[t][0] = op
            self.tok[t][1] = []
        deps.discard(op)
        op.deps, op.ddeps = [], {}
        for d in deps:
            if d.dsem is not None:
                op.ddeps[d.dsem] = self.dsems[d.dsem]
            elif d.eng == eng and eng == "pe":
                continue
            else:
                op.deps.append(d)
                d.signal = True
        if dsem is not None:
            self.dsems[dsem] = self.dsems.get(dsem, 0) + 16
        self.ops[eng].append(op)
        return op

    def assign(self):
        for e in self.ENG:
            c = 0
            for op in self.ops[e]:
                if op.dsem is None and op.signal:
                    c += 1
                    op.val = c

    def emit_engine(self, e, eng, sems, dh, final_waits=()):
        waited = {}
        for op in self.ops[e]:
            need = {}
            for d in op.deps:
                key = ("e", d.eng)
                if d.val > need.get(key, 0):
                    need[key] = d.val
            for k, v in op.ddeps.items():
                need[("d", k)] = v
            for key, v in need.items():
                if waited.get(key, 0) >= v:
                    continue
                waited[key] = v
                h = dh[key[1]] if key[0] == "d" else sems[key[1]]
                eng.wait_ge(h, v)
            ins = op.fn(eng)
            if op.dsem is not None:
                ins.then_inc(dh[op.dsem], 16)
            elif op.signal:
                ins.then_inc(sems[e], 1)
        for name in final_waits:
            eng.wait_ge(dh[name], self.dsems[name])


def _pipeline(n, stages, per_step=None, order=None):
    ns = len(stages)
    for step in range(n + ns - 1):
        for s in (order if order is not None else reversed(range(ns))):
            i = step - s
            if 0 <= i < n:
                stages[s](i)
        if per_step is not None:
            per_step(step, n + ns - 1)


def build_nc(NT=32, stages=9):
    T = NT * 128
    NST = NT // 4
    nc = bass.Bass("TRN2", target_bir_lowering=False)
    dt_in = lambda name, shape: nc.dram_tensor(name, shape, F32, kind="ExternalInput").ap()
    x_own = dt_in("x_own", [T, D])
    x_oth = dt_in("x_oth", [T, D])
    w_in = dt_in("w_in", [D, NIN])
    wdblk = dt_in("wdblk", [33, 512])
    consts = dt_in("consts", [128, NCONST])
    pvec = dt_in("pvec", [128, 24])
    rvec = dt_in("rvec", [1, 1024])
    gF = dt_in("gF", [D])
    wsT = dt_in("wsT", [128, 512])
    w_out = dt_in("w_out", [D, D])
    w_gate = dt_in("w_gate", [DFF, D])
    w_up = dt_in("w_up", [DFF, D])
    w_down = dt_in("w_down", [DFF, D])
    out = nc.dram_tensor("out", [T, D], F32, kind="ExternalOutput").ap()
    x1s = nc.dram_tensor("x1s", [T, D], F32).ap()

    S = Sched()
    dnames = set()

    def dma(outap, inap, reads, writes, dsem, add=None):
        dnames.add(dsem)
        return (add or S.add)("sp", lambda e: e.dma_start(out=outap, in_=inap), reads, writes, dsem)

    with ExitStack() as es:
        sb = lambda name, shape, dt: es.enter_context(nc.sbuf_tensor(name, shape, dt))
        PT = es.enter_context(nc.psum_tensor("PT", [128, 1024], BF16))
        PB = [es.enter_context(nc.psum_tensor("PB%d" % i, [128, 512], F32)) for i in range(7)]
        PA_, PB_, PC_, PD_, PE_, PF_, PG_ = PB
        cstI = sb("cstI", [128, 256], F32)
        idb = sb("idb", [128, 128], BF16)
        oneb = sb("oneb", [128, 128], BF16)
        pv = sb("pv", [128, 24], F32)
        epsc = sb("epsc", [128, 1], F32)
        bsc = sb("bsc", [128, 16], F32)
        stg = [sb("stg%d" % i, [128, 1024], F32) for i in range(3)]
        xs = [sb("xs%d" % i, [128, D], F32) for i in range(2)]
        xr = [sb("xr%d" % i, [128, D], F32) for i in range(2)]
        junk = sb("junk", [128, D], BF16)
        hb = [sb("hb%d" % i, [128, D], BF16) for i in range(2)]
        ss = [sb("ss%d" % i, [128, 2], F32) for i in range(2)]
        rstd = [sb("rstd%d" % i, [128, 1], F32) for i in range(2)]
        xT = sb("xT", [128, 8, 512], BF16)
        ARENA_F32 = 41984
        arena = sb("arena", [128, ARENA_F32], F32)
        cur = [0]

        def carve(ncols_elem, dt):
            nf = ncols_elem if dt == F32 else (ncols_elem + 1) // 2
            a = arena[:, cur[0]:cur[0] + nf]
            cur[0] += nf
            assert cur[0] <= ARENA_F32, cur[0]
            if dt != F32:
                a = a.bitcast(dt)
            return a

        cur[0] = 0
        winb = carve(8 * NIN, BF16).rearrange("p (k c) -> p k c", k=8)
        woutb = carve(8 * D, BF16).rearrange("p (k c) -> p k c", k=8)
        Sbst = carve(NT * 256, BF16).rearrange("p (j q v) -> p j q v", j=NT, q=2)
        cst = carve(NCONST, F32)
        wdb = carve(512, F32)[0:33, :]
        wsb = carve(512, BF16)
        Cg = carve(512, F32)
        lrT = [carve(128, F32)[0:33, :] for _ in range(4)]
        Sst = carve(256, F32).rearrange("p (q v) -> p q v", q=2)
        Sf = carve(256, F32).rearrange("p (q v) -> p q v", q=2)
        Sfb = carve(256, BF16).rearrange("p (q v) -> p q v", q=2)
        decs = [carve(2, F32) for _ in range(2)]
        mst = carve(8, F32)
        union0 = cur[0]
        rv = carve(1024, F32)[0:1, :]
        wsf = carve(512, F32)
        rsrow = carve(512, F32)[0:1, :]
        cur[0] = union0
        NSL = 4
        xTp = [carve(1024, BF16).rearrange("p (k t) -> p k t", k=8) for _ in range(NSL)]
        p_vb = [carve(512, BF16) for _ in range(NSL)]
        p_ktm = [carve(256, F32) for _ in range(NSL)]
        p_lr = [carve(32, F32) for _ in range(NSL)]
        p_lap = [carve(256, F32) for _ in range(NSL)]
        p_Eend = [carve(256, F32) for _ in range(NSL)]
        p_kte = [carve(256, BF16) for _ in range(NSL)]
        cur[0] = union0
        qT = carve(1024, F32).rearrange("p (q t) -> p q t", q=2)
        kT = carve(1024, F32).rearrange("p (q t) -> p q t", q=2)
        sgT = carve(2048, F32).rearrange("p (h t) -> p h t", h=4)
        uT = carve(2048, F32).rearrange("p (h t) -> p h t", h=4)
        vb = carve(4 * 512, BF16).rearrange("p (s c) -> p s c", s=4)
        gv = carve(4 * 512, F32).rearrange("p (s c) -> p s c", s=4)
        ktm = carve(4 * 256, F32).rearrange("p (s c) -> p s c", s=4)
        lrs = carve(4 * 32, F32).rearrange("p (s c) -> p s c", s=4)
        lnst = carve(4 * 2, F32).rearrange("p (s c) -> p s c", s=4)
        yT = carve(8 * 512, BF16).rearrange("p (k t) -> p k t", k=8)
        lap4 = [carve(512, F32) for _ in range(4)]
        lap0 = lap4[0]
        Ee0 = carve(512, F32)
        Ei0 = carve(512, F32)
        qdec0 = carve(512, BF16)
        kdec0 = carve(512, BF16)
        scF0 = carve(512, BF16)
        scB0 = carve(512, BF16)
        oTs0 = carve(512, F32)
        lap1, Ee1 = stg[0][:, 0:512], stg[0][:, 512:1024]
        Ei1 = stg[1][:, 0:512]
        qdec1 = stg[1][:, 512:768].bitcast(BF16)
        kdec1 = stg[1][:, 768:1024].bitcast(BF16)
        scF1 = stg[2][:, 0:256].bitcast(BF16)
        scB1 = stg[2][:, 256:512].bitcast(BF16)
        oTs1 = stg[2][:, 512:1024]
        GB = []
        for (lap_, Ee_, Ei_, qd_, kd_, sF_, sB_, oT_) in ((lap0, Ee0, Ei0, qdec0, kdec0, scF0, scB0, oTs0),
                                                         (lap1, Ee1, Ei1, qdec1, kdec1, scF1, scB1, oTs1)):
            GB.append(dict(lap=lap_, Ee=Ee_, Ei=Ei_, qdec=qd_.rearrange("p (d c) -> p d c", d=2),
                           kdec=kd_.rearrange("p (d c) -> p d c", d=2), scF=sF_, scB=sB_, oTs=oT_,
                           Eend=lap_[:, 0:256], rsb=Ei_, osq=sF_, kte=kd_[:, 0:256]))
        vn = carve(512, BF16)
        tmpg = carve(512, F32)
        endA = cur[0]
        cur[0] = 0
        wgb = carve(8 * DFF, BF16).rearrange("p (k c) -> p k c", k=8)
        wub = carve(8 * DFF, BF16).rearrange("p (k c) -> p k c", k=8)
        wdnb = carve(22 * D, BF16).rearrange("p (f c) -> p f c", f=22)
        actT = carve(22 * 512, BF16).rearrange("p (f t) -> p f t", f=22)
        sgs = [carve(512, F32) for _ in range(2)]
        gFb = carve(D, F32)
        mstB = carve(8, F32)
        endB = cur[0]
        assert max(endA, endB) <= ARENA_F32, (endA, endB)
        print("arena use (f32 cols): phaseA %d phaseB %d of %d" % (endA, endB, ARENA_F32))

        sems = {e: es.enter_context(nc.semaphore("s_" + e)) for e in Sched.ENG}

        bcount = [0]

        def barrier(extra_reads=()):
            n = bcount[0]
            bcount[0] += 1
            tag = "bar%d" % n
            S.add("act", lambda e: e.activation(out=bsc[:, 0:1], in_=epsc[:, 0:1], func=AF.Copy), ["epsc"], [tag + "a"])
            S.add("pool", lambda e: e.memset(bsc[:, 1:2], 0.0), [], [tag + "p"])
            S.add("pe", lambda e: e.matmul(PA_[0:1, 0:1], lhsT=cstI[:, 128:129], rhs=cstI[:, 128:129], start=True, stop=True),
                  ["cstI"], ["PA"])
            S.add("dve", lambda e: e.tensor_copy(out=bsc[0:1, 2:3], in_=PA_[0:1, 0:1]),
                  ["PA", tag + "a", tag + "p"] + list(extra_reads), [tag])
            S.add("act", lambda e: e.activation(out=bsc[:, 3:4], in_=epsc[:, 0:1], func=AF.Copy), ["epsc", tag], [tag + "ra"])
            S.add("pool", lambda e: e.memset(bsc[:, 4:5], 0.0), [tag], [tag + "rp"])
            S.add("pe", lambda e: e.matmul(PA_[0:1, 0:1], lhsT=cstI[:, 128:129], rhs=cstI[:, 128:129], start=True, stop=True),
                  ["cstI", tag], ["PA"])
            return tag

        dma(cstI[:], consts[:, 0:256], [], ["cstI"], "dconst")
        dma(cst[:], consts, [], ["cst"], "dconst")
        dma(pv[:], pvec, [], ["pv"], "dconst")
        dma(rv, rvec, [], ["rv"], "dconst")
        dma(wdb, wdblk, [], ["wdb"], "dconst")
        dma(wsf, wsT, [], ["wsf"], "dconst")
        S.add("dve", lambda e: e.memset(epsc[:], EPS), [], ["epsc"])
        S.add("dve", lambda e: e.tensor_copy(out=idb[:], in_=cstI[:, 0:128]), ["cstI"], ["idb"])
        S.add("dve", lambda e: e.tensor_copy(out=oneb[:], in_=cstI[:, 128:256]), ["cstI"], ["oneb"])
        S.add("dve", lambda e: e.tensor_copy(out=wsb, in_=wsf), ["wsf"], ["wsb"])
        for i in range(4):
            S.add("dve", lambda e, i=i: e.memset(lrT[i][32:33, :], 1.0), [], ["lrT1_%d" % i])
        S.add("dve", lambda e: e.memset(Sst, 0.0), [], ["Sst"])
        S.add("pe", lambda e: e.matmul(PA_[0:1, :], lhsT=cstI[:, 128:129], rhs=wsf, start=True, stop=True),
              ["cstI", "wsf"], ["PA"])
        S.add("dve", lambda e: e.tensor_copy(out=rsrow, in_=PA_[0:1, :]), ["PA"], ["rsrow"])
        for g in range(4):
            S.add("pe", lambda e, g=g: e.matmul(PB_[:, g * 128:(g + 1) * 128], lhsT=rv[0:1, g * 128:(g + 1) * 128],
                                                rhs=rsrow[0:1, g * 128:(g + 1) * 128], start=True, stop=False),
                  ["rv", "rsrow"], ["PB"])
            S.add("pe", lambda e, g=g: e.matmul(PB_[:, g * 128:(g + 1) * 128], lhsT=cstI[0:1, 128:256],
                                                rhs=rv[0:1, 512 + g * 128:512 + (g + 1) * 128], start=False, stop=True),
                  ["rv", "cstI"], ["PB"])
        S.add("dve", lambda e: e.tensor_copy(out=Cg, in_=PB_[:, :]), ["PB"], ["Cg"])

        wl = [0]

        def weight_chunks(dst3, src2, ktiles, col_ranges, scale_col0=None, tokname="w", extra_reads=()):
            th = []
            for k in range(ktiles):
                for (lo, hi) in col_ranges:
                    c0 = lo
                    while c0 < hi:
                        cw = min(1024, hi - c0)

                        def thunk(k=k, c0=c0, cw=cw):
                            sl = wl[0] % 3
                            wl[0] += 1
                            use_act = (wl[0] % 2) == 0
                            dma(stg[sl][:, 0:cw], src2[k * 128:(k + 1) * 128, c0:c0 + cw], list(extra_reads), ["stg%d" % sl], "dstg%d" % sl)
                            rd = ["stg%d" % sl] + list(extra_reads)
                            if scale_col0 is None:
                                if use_act:
                                    S.add("act", lambda e: e.activation(out=dst3[:, k, c0:c0 + cw], in_=stg[sl][:, 0:cw], func=AF.Copy), rd, [tokname])
                                else:
                                    S.add("dve", lambda e: e.tensor_copy(out=dst3[:, k, c0:c0 + cw], in_=stg[sl][:, 0:cw]), rd, [tokname])
                            else:
                                sc = pv[:, scale_col0 + k:scale_col0 + k + 1]
                                if use_act:
                                    S.add("act", lambda e: e.activation(out=dst3[:, k, c0:c0 + cw], in_=stg[sl][:, 0:cw], func=AF.Copy, scale=sc),
                                          rd + ["pv"], [tokname])
                                else:
                                    S.add("dve", lambda e: e.tensor_scalar(out=dst3[:, k, c0:c0 + cw], in0=stg[sl][:, 0:cw], scalar1=sc,
                                                                           scalar2=None, op0=ALU.mult), rd + ["pv"], [tokname])
                        th.append(thunk)
                        c0 += cw
            return th

        barrier()
        for t_ in weight_chunks(winb, w_in, 8, [(NFM, NFM + 800)], tokname="winb_tm"):
            t_()
        late_w = (weight_chunks(winb, w_in, 8, [(0, NFM), (NFM + 800, NIN)], tokname="winb_rest")
                  + weight_chunks(woutb, w_out, 8, [(0, D)], tokname="woutb"))

        def late_hook(step, nsteps):
            want = (step + 1) * len(late_w) // max(1, nsteps - 7)
            while late_hook.i < min(want, len(late_w)):
                late_w[late_hook.i]()
                late_hook.i += 1
        late_hook.i = 0

        pc = [0]

        def prep_load(src_rows, src_reads=(), add=None):
            sl = pc[0] % 2
            pc[0] += 1
            dma(xs[sl][:], src_rows, list(src_reads), ["xs%d" % sl], "dxs%d" % sl, add=add)
            add = add or S.add
            add("act", lambda e: e.activation(out=junk[:], in_=xs[sl][:], func=AF.Square, accum_out=ss[sl][:, 0:1]),
                  ["xs%d" % sl], ["junk", "ss%d" % sl])
            add("act", lambda e: e.activation(out=ss[sl][:, 1:2], in_=ss[sl][:, 0:1], func=AF.Ln, scale=1.0 / D, bias=epsc[:, 0:1]),
                  ["ss%d" % sl, "epsc"], ["ss%d" % sl])
            add("act", lambda e: e.activation(out=rstd[sl][:], in_=ss[sl][:, 1:2], func=AF.Exp, scale=-0.5),
                  ["ss%d" % sl], ["rstd%d" % sl])
            add("act", lambda e: e.activation(out=hb[sl][:], in_=xs[sl][:], func=AF.Copy, scale=rstd[sl][:, 0:1]),
                  ["xs%d" % sl, "rstd%d" % sl], ["hb%d" % sl])
            return sl

        def prep_transpose(sl, dst3, dtok, gcol=0, add=None):
            add = add or S.add
            for k in range(8):
                add("pe", lambda e, k=k: e.transpose(out=PT[:, k * 128:(k + 1) * 128], in_=hb[sl][:, k * 128:(k + 1) * 128], identity=idb[:]),
                      ["hb%d" % sl, "idb"], ["PT"])
            add("dve", lambda e: e.tensor_tensor(out=dst3, in0=PT[:].rearrange("p (k t) -> p k t", k=8),
                                                   in1=pv[:, gcol:gcol + 8].unsqueeze(2).to_broadcast([128, 8, 128]), op=ALU.mult),
                  ["PT", "pv"], [dtok])

        def la_from_lr(lr_ap, lr_tok, slot, zcols, lap_ap, lap_tok, zbank, zbank_tok, add=None):
            add = add or S.add
            add("pe", lambda e: e.transpose(out=zbank[0:32, 0:128], in_=lr_ap, identity=cstI[:, 0:128]),
                [lr_tok, "cstI"], [zbank_tok])
            add("dve", lambda e: e.tensor_copy(out=lrT[slot][0:32, :], in_=zbank[0:32, 0:128]), [zbank_tok], ["lrT%d" % slot])
            c0, c1 = zcols
            add("pe", lambda e: e.matmul(zbank[:, c0:c1], lhsT=lrT[slot][:, :], rhs=wdb[:, c0:c1], start=True, stop=True),
                ["lrT%d" % slot, "lrT1_%d" % slot, "wdb"], [zbank_tok])
            add("act", lambda e: e.activation(out=lap_ap, in_=zbank[:, c0:c1], func=AF.Exp, scale=-1.0), [zbank_tok], [lap_tok])
            add("act", lambda e: e.activation(out=lap_ap, in_=lap_ap, func=AF.Ln, bias=1.0), [lap_tok], [lap_tok])

        def state_update(Sm, Sm_tok, dps, dps_tok, dec_ap, dec_tok, add=None):
            add = add or S.add
            for p in range(2):
                for r in range(2):
                    add("dve", lambda e, p=p, r=r: e.scalar_tensor_tensor(
                        out=Sm[r * 64:(r + 1) * 64, p, :], in0=Sm[r * 64:(r + 1) * 64, p, :],
                        scalar=dec_ap[r * 64:(r + 1) * 64, p:p + 1],
                        in1=dps[r * 64:(r + 1) * 64, p * 256 + r * 128:p * 256 + (r + 1) * 128],
                        op0=ALU.mult, op1=ALU.add), [Sm_tok, dps_tok, dec_tok], [Sm_tok])

        def scan_items(src, order, dirn, store, Sm, Sm_tok):
            zc = (dirn * 256, dirn * 256 + 256)
            tmat = C_LSN if dirn == 0 else C_USN
            return [(src, j, zc, tmat, store, Sm, Sm_tok) for j in order]

        items = []
        if stages >= 1:
            it1 = scan_items(x_oth, list(range(NT)), 0, False, Sf, "Sf")
            it2 = scan_items(x_own, list(range(NT - 1, -1, -1)), 1, True, Sst, "Sst") if stages >= 2 else []
            for a_, b_ in zip(it1, it2 if it2 else [None] * len(it1)):
                items.append(a_)
                if b_ is not None:
                    items.append(b_)
        S.add("dve", lambda e: e.memset(Sf, 0.0), [], ["Sf"])
        sls = {}

        def st0(i):
            src, j = items[i][0], items[i][1]
            sls[i] = prep_load(src[j * 128:(j + 1) * 128, :])

        def st1a(i):
            s = i % NSL
            prep_transpose(sls[i], xTp[s], "xTp%d" % s)

        def st1b(i):
            s = i % NSL
            for k in range(8):
                S.add("pe", lambda e, k=k: e.matmul(PA_[:, 0:288], lhsT=xTp[s][:, k, :], rhs=winb[:, k, NFM:NFM + 288],
                                                    start=(k == 0), stop=(k == 7)), ["xTp%d" % s, "winb_tm"], ["PA"])
            for k in range(8):
                S.add("pe", lambda e, k=k: e.matmul(PB_[:, :], lhsT=xTp[s][:, k, :], rhs=winb[:, k, NFM + 288:NFM + 800],
                                                    start=(k == 0), stop=(k == 7)), ["xTp%d" % s, "winb_tm"], ["PB"])
            S.add("act", lambda e: e.activation(out=p_ktm[s], in_=PA_[:, 0:256], func=AF.Copy), ["PA"], ["p_ktm%d" % s])
            S.add("act", lambda e: e.activation(out=p_lr[s], in_=PA_[:, 256:288], func=AF.Copy), ["PA"], ["p_lr%d" % s])
            S.add("dve", lambda e: e.tensor_copy(out=p_vb[s], in_=PB_[:, :]), ["PB"], ["p_vb%d" % s])

        def st1c1(i):
            s = i % NSL
            q = i % 2
            S.add("pe", lambda e: e.transpose(out=PC_[0:32, 0:128], in_=p_lr[s], identity=cstI[:, 0:128]), ["p_lr%d" % s, "cstI"], ["PC"])
            S.add("dve", lambda e: e.tensor_copy(out=lrT[q][0:32, :], in_=PC_[0:32, 0:128]), ["PC"], ["lrT%d" % q])

        def st1c2(i):
            s = i % NSL
            q = i % 2
            c0, c1 = items[i][2]
            S.add("pe", lambda e: e.matmul(PC_[:, c0:c1], lhsT=lrT[q][:, :], rhs=wdb[:, c0:c1], start=True, stop=True),
                  ["lrT%d" % q, "lrT1_%d" % q, "wdb"], ["PC"])
            S.add("act", lambda e: e.activation(out=p_lap[s], in_=PC_[:, c0:c1], func=AF.Exp, scale=-1.0), ["PC"], ["p_lap%d" % s])
            S.add("act", lambda e: e.activation(out=p_lap[s], in_=p_lap[s], func=AF.Ln, bias=1.0), ["p_lap%d" % s], ["p_lap%d" % s])

        def st2a(i):
            s = i % NSL
            d = i % 2
            src, j, zc, tmat, store, Sm, Sm_tok = items[i]
            S.add("pe", lambda e: e.matmul(PD_[:, 0:256], lhsT=cst[:, tmat:tmat + 128], rhs=p_lap[s], start=True, stop=True),
                  ["cst", "p_lap%d" % s], ["PD"])
            for p in range(2):
                S.add("pe", lambda e, p=p: e.matmul(PD_[:, 256 + p:257 + p], lhsT=p_lap[s][:, p * 128:(p + 1) * 128],
                                                    rhs=cst[:, C_UN + 127:C_UN + 128], start=True, stop=True),
                      ["cst", "p_lap%d" % s], ["PD"])
            S.add("act", lambda e: e.activation(out=p_Eend[s], in_=PD_[:, 0:256], func=AF.Exp), ["PD"], ["p_Eend%d" % s])
            S.add("act", lambda e: e.activation(out=decs[d], in_=PD_[:, 256:258], func=AF.Exp), ["PD"], ["decs%d" % d])
            S.add("dve", lambda e: e.tensor_tensor(out=p_kte[s], in0=p_ktm[s], in1=p_Eend[s], op=ALU.mult),
                  ["p_ktm%d" % s, "p_Eend%d" % s], ["p_kte%d" % s])

        def st2b(i):
            s = i % NSL
            d = i % 2
            src, j, zc, tmat, store, Sm, Sm_tok = items[i]
            for p in range(2):
                S.add("pe", lambda e, p=p: e.matmul(PE_[:, p * 256:(p + 1) * 256], lhsT=p_kte[s][:, p * 128:(p + 1) * 128],
                                                    rhs=p_vb[s][:, p * 256:(p + 1) * 256], start=True, stop=True),
                      ["p_kte%d" % s, "p_vb%d" % s], ["PE"])
            if store:
                S.add("act", lambda e: e.activation(out=Sbst[:, j, :, :], in_=Sm, func=AF.Copy), [Sm_tok], ["Sbst%d" % j])
            state_update(Sm, Sm_tok, PE_, "PE", decs[d], "decs%d" % d)

        _pipeline(len(items), [st0, st1a, st1b, st1c1, st1c2, st2a, st2b], per_step=late_hook, order=[1, 3, 6, 5, 2, 4, 0])
        while late_hook.i < len(late_w):
            late_w[late_hook.i]()
            late_hook.i += 1
        S.add("act", lambda e: e.activation(out=Sfb, in_=Sf, func=AF.Copy), ["Sf"], ["Sfb"])
        barrier()

        xrc = [0]
        sub_sl = {}

        def m0(i):
            for sub in range(4):
                j = i * 4 + sub
                sub_sl[j] = prep_load(x_own[j * 128:(j + 1) * 128, :])
                prep_transpose(sub_sl[j], xT[:, :, sub * 128:(sub + 1) * 128], "xT")

        def m1(i):
            fm_banks = [(PF_, "PF"), (PG_, "PG")]
            for c in range(12):
                bank, btok = fm_banks[c % 2]
                for k in range(8):
                    S.add("pe", lambda e, c=c, k=k, bank=bank: e.matmul(bank[:, :], lhsT=winb[:, k, c * 128:(c + 1) * 128], rhs=xT[:, k, :],
                                                                       start=(k == 0), stop=(k == 7)), ["winb_tm", "winb_rest", "xT"], [btok])
                if c < 2:
                    S.add("act", lambda e, c=c, bank=bank: e.activation(out=qT[:, c, :], in_=bank[:, :], func=AF.Copy), [btok], ["qT"])
                elif c < 4:
                    S.add("act", lambda e, c=c, bank=bank: e.activation(out=kT[:, c - 2, :], in_=bank[:, :], func=AF.Copy), [btok], ["kT"])
                elif c < 8:
                    S.add("act", lambda e, c=c, bank=bank: e.activation(out=sgT[:, c - 4, :], in_=bank[:, :], func=AF.Silu), [btok], ["sgT"])
                else:
                    S.add("act", lambda e, c=c, bank=bank: e.activation(out=uT[:, c - 8, :], in_=bank[:, :], func=AF.Gelu), [btok], ["uT"])
            for sub in range(4):
                tsl = slice(sub * 128, (sub + 1) * 128)
                for (bank, btok, c0, cw) in ((PA_, "PA", 0, 288), (PB_, "PB", 288, 512), (PC_, "PC", 800, 512)):
                    for k in range(8):
                        S.add("pe", lambda e, k=k, bank=bank, c0=c0, cw=cw, tsl=tsl: e.matmul(
                            bank[:, 0:cw], lhsT=xT[:, k, tsl], rhs=winb[:, k, NFM + c0:NFM + c0 + cw],
                            start=(k == 0), stop=(k == 7)), ["winb_tm", "winb_rest", "xT"], [btok])
                S.add("dve", lambda e, sub=sub: e.tensor_copy(out=ktm[:, sub, :], in_=PA_[:, 0:256]), ["PA"], ["ktm%d" % sub])
                S.add("dve", lambda e, sub=sub: e.tensor_copy(out=lrs[:, sub, :], in_=PA_[:, 256:288]), ["PA"], ["lrs%d" % sub])
                S.add("dve", lambda e, sub=sub: e.tensor_copy(out=vb[:, sub, :], in_=PB_[:, :]), ["PB"], ["vb%d" % sub])
                S.add("act", lambda e, sub=sub: e.activation(out=gv[:, sub, :], in_=PC_[:, :], func=AF.Gelu,
                                                             accum_out=lnst[:, sub, 0:1]), ["PC"], ["gv%d" % sub, "lnst%d" % sub])
            las = []
            for sub in range(4):
                l_ = []
                la_from_lr(lrs[:, sub, :], "lrs%d" % sub, sub, (0, 512), lap4[sub], "lap4_%d" % sub,
                           (PA_, PB_, PC_, PD_)[sub], ("PA", "PB", "PC", "PD")[sub],
                           add=lambda *a, l_=l_, **k: l_.append(lambda: S.add(*a, **k)))
                las.append(l_)
            for k_ in range(len(las[0])):
                for sub in range(4):
                    las[sub][k_]()
            glas, gms, ops_ = [], [], []
            for sub in range(4):
                g_, m_, o_ = [], [], []
                gla_sub(i, sub, lambda *a, g_=g_, **k: g_.append(lambda: S.add(*a, **k)))
                gmlp_sub(i, sub, lambda *a, m_=m_, **k: m_.append(lambda: S.add(*a, **k)))
                if sub > 0:
                    outproj_sub(i, sub - 1, lambda *a, o_=o_, **k: o_.append(lambda: S.add(*a, **k)))
                if i + 1 < NST:
                    jn = (i + 1) * 4 + sub
                    A_ = lambda *a, o_=o_, **k: o_.append(lambda: S.add(*a, **k))
                    sln = prep_load(x_own[jn * 128:(jn + 1) * 128, :], add=A_)
                    prep_transpose(sln, xT[:, :, sub * 128:(sub + 1) * 128], "xT", add=A_)
                glas.append(g_)
                gms.append(m_)
                ops_.append(o_)
            L = len(glas[0])
            H = (L + 1) // 2
            mi = [0] * 4
            oi = [0] * 4
            for tick in range(H * 3 + L):
                for sub in range(4):
                    k = tick - sub * H
                    if 0 <= k < L:
                        glas[sub][k]()
                        if k < H:
                            want = (k + 1) * len(gms[sub]) // H
                            while mi[sub] < want:
                                gms[sub][mi[sub]]()
                                mi[sub] += 1
                        else:
                            want = (k + 1 - H) * len(ops_[sub]) // (L - H)
                            while oi[sub] < want:
                                ops_[sub][oi[sub]]()
                                oi[sub] += 1
            for sub in range(4):
                assert mi[sub] == len(gms[sub]) and oi[sub] == len(ops_[sub])
            outproj_sub(i, 3, S.add)

        def gmlp_sub(i, sub, A):
            tsl = slice(sub * 128, (sub + 1) * 128)
            A("act", lambda e: e.activation(out=vn, in_=gv[:, sub, :], func=AF.Square,
                                            accum_out=lnst[:, sub, 1:2]), ["gv%d" % sub], ["vn", "lnst%d" % sub])
            A("dve", lambda e: e.tensor_scalar(out=mst[:, 0:2], in0=lnst[:, sub, :], scalar1=1.0 / 512, scalar2=None, op0=ALU.mult),
              ["lnst%d" % sub], ["mst"])
            A("dve", lambda e: e.tensor_tensor(out=mst[:, 2:3], in0=mst[:, 0:1], in1=mst[:, 0:1], op=ALU.mult), ["mst"], ["mst"])
            A("dve", lambda e: e.tensor_tensor(out=mst[:, 3:4], in0=mst[:, 1:2], in1=mst[:, 2:3], op=ALU.subtract), ["mst"], ["mst"])
            A("act", lambda e: e.activation(out=mst[:, 4:5], in_=mst[:, 3:4], func=AF.Ln, bias=epsc[:, 0:1]), ["mst", "epsc"], ["mst"])
            A("act", lambda e: e.activation(out=mst[:, 5:6], in_=mst[:, 4:5], func=AF.Exp, scale=-0.5), ["mst"], ["mst"])
            A("dve", lambda e: e.tensor_scalar(out=vn, in0=gv[:, sub, :], scalar1=mst[:, 0:1], scalar2=mst[:, 5:6],
                                               op0=ALU.subtract, op1=ALU.mult), ["gv%d" % sub, "mst"], ["vn"])
            for g in range(4):
                A("pe", lambda e, g=g: e.matmul(PF_[:, g * 128:(g + 1) * 128], lhsT=vn[:, g * 128:(g + 1) * 128],
                                                rhs=wsb[:, g * 128:(g + 1) * 128], start=True, stop=True), ["vn", "wsb"], ["PF"])
            for g in range(4):
                A("dve", lambda e, g=g: e.scalar_tensor_tensor(
                    out=tmpg[:, g * 128:(g + 1) * 128], in0=PF_[:, g * 128:(g + 1) * 128], scalar=pv[:, 20 + g:21 + g],
                    in1=Cg[:, g * 128:(g + 1) * 128], op0=ALU.mult, op1=ALU.add), ["PF", "pv", "Cg"], ["tmpg"])
            A("dve", lambda e: e.tensor_tensor(out=yT[:, 4:8, tsl], in0=tmpg.rearrange("p (g t) -> p g t", g=4),
                                               in1=uT[:, :, tsl], op=ALU.mult), ["tmpg", "uT"], ["yTb%d" % sub])

        def gla_sub(i, sub, A):
            j = i * 4 + sub
            q = sub % 2
            B_ = GB[q]
            lap, Ee, Ei, qdec, kdec, scF, scB, oTs = (B_[n] for n in ("lap", "Ee", "Ei", "qdec", "kdec", "scF", "scB", "oTs"))
            Eend, rsb, osq, kte = B_["Eend"], B_["rsb"], B_["osq"], B_["kte"]
            T = lambda n: "%s_%d" % (n, q)
            tsl = slice(sub * 128, (sub + 1) * 128)
            lap = lap4[sub]
            Eend = lap[:, 0:256]
            for dirn in range(2):
                um = C_UN if dirn == 0 else C_UTN
                for p in range(2):
                    A("pe", lambda e, dirn=dirn, p=p, um=um: e.matmul(
                        PB_[:, (dirn * 2 + p) * 128:(dirn * 2 + p + 1) * 128], lhsT=lap[:, dirn * 256 + p * 128:dirn * 256 + (p + 1) * 128],
                        rhs=cst[:, um:um + 128], start=True, stop=True), ["lap4_%d" % sub, "cst"], ["PB"])
            A("act", lambda e: e.activation(out=Ee, in_=PB_[:, :], func=AF.Exp), ["PB"], [T("Ee")])
            A("act", lambda e: e.activation(out=Ei, in_=PB_[:, :], func=AF.Exp, scale=-1.0), ["PB"], [T("Ei")])
            for dirn in range(2):
                A("dve", lambda e, dirn=dirn: e.scalar_tensor_tensor(
                    out=qdec[:, dirn, :].rearrange("p (q t) -> p q t", q=2), in0=qT[:, :, tsl], scalar=0.125,
                    in1=Ee[:, dirn * 256:(dirn + 1) * 256].rearrange("p (q t) -> p q t", q=2), op0=ALU.mult, op1=ALU.mult),
                    ["qT", T("Ee")], [T("qdec")])
                A("dve", lambda e, dirn=dirn: e.tensor_tensor(
                    out=kdec[:, dirn, :].rearrange("p (q t) -> p q t", q=2), in0=kT[:, :, tsl],
                    in1=Ei[:, dirn * 256:(dirn + 1) * 256].rearrange("p (q t) -> p q t", q=2), op=ALU.mult), ["kT", T("Ei")], [T("kdec")])
            for dirn in range(2):
                for p in range(2):
                    for r, (bank, btok) in enumerate(((PC_, "PC"), (PD_, "PD"))):
                        A("pe", lambda e, dirn=dirn, p=p, r=r, bank=bank: e.matmul(
                            bank[:, (dirn * 2 + p) * 128:(dirn * 2 + p + 1) * 128], lhsT=kdec[r * 64:(r + 1) * 64, dirn, p * 128:(p + 1) * 128],
                            rhs=qdec[r * 64:(r + 1) * 64, dirn, p * 128:(p + 1) * 128], start=True, stop=True),
                            [T("kdec"), T("qdec")], [btok])
            A("dve", lambda e: e.tensor_tensor(out=scF, in0=PC_[:, :], in1=cst[:, C_MF + 256:C_MF + 768], op=ALU.mult), ["PC", "cst"], [T("scF")])
            A("dve", lambda e: e.tensor_tensor(out=scB, in0=PD_[:, :], in1=cst[:, C_MF + 256:C_MF + 768], op=ALU.mult), ["PD", "cst"], [T("scB")])
            A("pe", lambda e: e.matmul(PA_[:, 0:256], lhsT=cst[:, C_LSN:C_LSN + 128], rhs=lap[:, 0:256], start=True, stop=True),
              ["cst", "lap4_%d" % sub], ["PA"])
            A("act", lambda e: e.activation(out=Eend, in_=PA_[:, 0:256], func=AF.Exp), ["PA"], ["lap4_%d" % sub])
            A("dve", lambda e: e.tensor_tensor(out=kte, in0=ktm[:, sub, :], in1=Eend, op=ALU.mult), ["ktm%d" % sub, "lap4_%d" % sub], [T("kdec")])
            for h in range(4):
                p, r = h // 2, h % 2
                osl = slice(h * 128, (h + 1) * 128)
                scr = scF if r == 0 else scB
                A("pe", lambda e, p=p, scr=scr, osl=osl: e.matmul(PE_[:, osl], lhsT=vb[:, sub, osl], rhs=scr[:, p * 128:(p + 1) * 128],
                                                                  start=True, stop=False), ["vb%d" % sub, T("scF"), T("scB")], ["PE"])
                A("pe", lambda e, p=p, scr=scr, osl=osl: e.matmul(PE_[:, osl], lhsT=vb[:, sub, osl], rhs=scr[:, (2 + p) * 128:(3 + p) * 128],
                                                                  start=False, stop=False), ["vb%d" % sub, T("scF"), T("scB")], ["PE"])
                A("pe", lambda e, p=p, r=r, osl=osl: e.matmul(PE_[:, osl], lhsT=Sfb[r * 64:(r + 1) * 64, p, :],
                                                              rhs=qdec[r * 64:(r + 1) * 64, 0, p * 128:(p + 1) * 128], start=False, stop=False),
                  ["Sfb", T("qdec")], ["PE"])
                A("pe", lambda e, p=p, r=r, osl=osl: e.matmul(PE_[:, osl], lhsT=Sbst[r * 64:(r + 1) * 64, j, p, :],
                                                              rhs=qdec[r * 64:(r + 1) * 64, 1, p * 128:(p + 1) * 128], start=False, stop=True),
                  ["Sbst%d" % j, T("qdec")], ["PE"])
            A("dve", lambda e: e.tensor_copy(out=oTs, in_=PE_[:, :]), ["PE"], [T("oTs")])
            A("act", lambda e: e.activation(out=osq, in_=oTs, func=AF.Square), [T("oTs")], [T("scF")])
            A("pe", lambda e: e.matmul(PE_[:, :], lhsT=oneb[:], rhs=osq, start=True, stop=True), ["oneb", T("scF")], ["PE"])
            A("act", lambda e: e.activation(out=rsb, in_=PE_[:, :], func=AF.Ln, scale=1.0 / 128, bias=epsc[:, 0:1]), ["PE", "epsc"], [T("Ei")])
            for p in range(2):
                A("pe", lambda e, p=p: e.matmul(PE_[:, p * 256:(p + 1) * 256], lhsT=kte[:, p * 128:(p + 1) * 128],
                                                rhs=vb[:, sub, p * 256:(p + 1) * 256], start=True, stop=True), [T("kdec"), "vb%d" % sub], ["PE"])
            A("act", lambda e: e.activation(out=rsb, in_=rsb, func=AF.Exp, scale=-0.5), [T("Ei")], [T("Ei")])
            A("dve", lambda e: e.tensor_copy(out=decs[q], in_=Ee[:, 127:256:128]), [T("Ee")], ["decs%d" % q])
            state_update(Sf, "Sf", PE_, "PE", decs[q], "decs%d" % q, add=A)
            A("act", lambda e: e.activation(out=Sfb, in_=Sf, func=AF.Copy), ["Sf"], ["Sfb"])
            A("dve", lambda e: e.tensor_tensor(out=oTs, in0=oTs, in1=rsb, op=ALU.mult), [T("oTs"), T("Ei")], [T("oTs")])
            for h in range(4):
                A("dve", lambda e, h=h: e.scalar_tensor_tensor(
                    out=yT[:, h, tsl], in0=oTs[:, h * 128:(h + 1) * 128], scalar=pv[:, 16 + h:17 + h],
                    in1=sgT[:, h, tsl], op0=ALU.mult, op1=ALU.mult), [T("oTs"), "pv", "sgT"], ["yTa%d" % sub])

        def outproj_sub(i, sub, A):
            j = i * 4 + sub
            tsl = slice(sub * 128, (sub + 1) * 128)
            xsl = xrc[0] % 2
            xrc[0] += 1
            dma(xr[xsl][:], x_own[j * 128:(j + 1) * 128, :], [], ["xr%d" % xsl], "dxr%d" % xsl, add=A)
            for n, (bank, btok) in enumerate(((PF_, "PF"), (PG_, "PG"))):
                for k in range(8):
                    A("pe", lambda e, n=n, k=k, bank=bank: e.matmul(bank[:, :], lhsT=yT[:, k, tsl], rhs=woutb[:, k, n * 512:(n + 1) * 512],
                                                                   start=(k == 0), stop=(k == 7)), ["yTa%d" % sub, "yTb%d" % sub, "woutb"], [btok])
                A("dve", lambda e, n=n, bank=bank: e.tensor_tensor(out=xr[xsl][:, n * 512:(n + 1) * 512], in0=bank[:, :],
                                                                   in1=xr[xsl][:, n * 512:(n + 1) * 512], op=ALU.add),
                  [btok, "xr%d" % xsl], ["xr%d" % xsl])
            dma(x1s[j * 128:(j + 1) * 128, :], xr[xsl][:], ["xr%d" % xsl], ["x1s_%d" % j], "dx1_%d" % xsl, add=A)

        if stages >= 2.5:
            m0(0)
            for i_ in range(NST):
                m1(i_)

        btag = barrier()
        dma(gFb, gF.partition_broadcast(128), [btag], ["gFb"], "dconst2")
        def gu_chunk(f, dst3, src2, tok, use_act):
            sl = wl[0] % 3
            wl[0] += 1
            dma(stg[sl][:], src2[f * 128:(f + 1) * 128, :], [btag], ["stg%d" % sl], "dstg%d" % sl)
            o3 = dst3[:, :, f * 128:(f + 1) * 128]
            i3 = stg[sl][:].rearrange("p (k c) -> p k c", k=8)
            if use_act:
                S.add("act", lambda e: e.activation(out=o3, in_=i3, func=AF.Copy), ["stg%d" % sl, btag], ["%s%d" % (tok, f)])
            else:
                S.add("dve", lambda e: e.tensor_copy(out=o3, in_=i3), ["stg%d" % sl, btag], ["%s%d" % (tok, f)])

        def dn_chunk(f):
            sl = wl[0] % 3
            wl[0] += 1
            dma(stg[sl][:], w_down[f * 128:(f + 1) * 128, :], [btag], ["stg%d" % sl], "dstg%d" % sl)
            if f % 2 == 0:
                S.add("act", lambda e: e.activation(out=wdnb[:, f, :], in_=stg[sl][:], func=AF.Copy), ["stg%d" % sl, btag], ["wdn%d" % f])
            else:
                S.add("dve", lambda e: e.tensor_copy(out=wdnb[:, f, :], in_=stg[sl][:]), ["stg%d" % sl, btag], ["wdn%d" % f])

        fsl = {}

        def f0(i):
            for sub in range(4):
                j = i * 4 + sub
                fsl[j] = prep_load(x1s[j * 128:(j + 1) * 128, :], ["x1s_%d" % j])
                prep_transpose(fsl[j], xT[:, :, sub * 128:(sub + 1) * 128], "xT", gcol=8)

        oc = [0]

        def f1(i):
            gb = [(PA_, "PA"), (PB_, "PB")]
            ub = [(PC_, "PC"), (PD_, "PD")]
            for f in range(22):
                gbank, gtok = gb[f % 2]
                ubank, utok = ub[f % 2]
                if i == 0:
                    gu_chunk(f, wgb, w_gate, "wg", True)
                    gu_chunk(f, wub, w_up, "wu", False)
                    if f >= 11:
                        dn_chunk(2 * (f - 11))
                        dn_chunk(2 * (f - 11) + 1)
                for k in range(8):
                    S.add("pe", lambda e, f=f, k=k, gbank=gbank: e.matmul(gbank[:, :], lhsT=wgb[:, k, f * 128:(f + 1) * 128], rhs=xT[:, k, :],
                                                                         start=(k == 0), stop=(k == 7)), ["wg%d" % f, "xT"], [gtok])
                for k in range(8):
                    S.add("pe", lambda e, f=f, k=k, ubank=ubank: e.matmul(ubank[:, :], lhsT=wub[:, k, f * 128:(f + 1) * 128], rhs=xT[:, k, :],
                                                                         start=(k == 0), stop=(k == 7)), ["wu%d" % f, "xT"], [utok])
                S.add("act", lambda e, f=f, gbank=gbank: e.activation(out=sgs[f % 2], in_=gbank[:, :], func=AF.Silu), [gtok], ["sgs%d" % (f % 2)])
                S.add("dve", lambda e, f=f, ubank=ubank: e.tensor_tensor(out=actT[:, f, :], in0=ubank[:, :], in1=sgs[f % 2], op=ALU.mult),
                      [utok, "sgs%d" % (f % 2)], ["actT"])
            nxt = i + 1 < NST
            nsl = {}

            def nprep_load(sub):
                jn = (i + 1) * 4 + sub
                nsl[sub] = prep_load(x1s[jn * 128:(jn + 1) * 128, :], ["x1s_%d" % jn])

            if nxt:
                nprep_load(0)
                nprep_load(1)
            for sub in range(4):
                j = i * 4 + sub
                tsl = slice(sub * 128, (sub + 1) * 128)
                xsl = xrc[0] % 2
                xrc[0] += 1
                dma(xr[xsl][:], x1s[j * 128:(j + 1) * 128, :], ["x1s_%d" % j], ["xr%d" % xsl], "dxr%d" % xsl)
                dbanks = ((PE_, "PE"), (PF_, "PF")) if sub % 2 == 0 else ((PG_, "PG"), (PA_, "PA"))
                for n, (bank, btok) in enumerate(dbanks):
                    for f in range(22):
                        S.add("pe", lambda e, n=n, f=f, bank=bank, tsl=tsl: e.matmul(bank[:, :], lhsT=actT[:, f, tsl], rhs=wdnb[:, f, n * 512:(n + 1) * 512],
                                                                                    start=(f == 0), stop=(f == 21)), ["actT", "wdn%d" % f], [btok])
                    S.add("dve", lambda e, n=n, bank=bank, xsl=xsl: e.tensor_tensor(out=xr[xsl][:, n * 512:(n + 1) * 512], in0=bank[:, :],
                                                                                   in1=xr[xsl][:, n * 512:(n + 1) * 512], op=ALU.add),
                          [btok, "xr%d" % xsl], ["xr%d" % xsl])
                if nxt:
                    prep_transpose(nsl[sub], xT[:, :, sub * 128:(sub + 1) * 128], "xT", gcol=8)
                    if sub + 2 < 4:
                        nprep_load(sub + 2)
                S.add("act", lambda e, xsl=xsl: e.activation(out=junk[:], in_=xr[xsl][:], func=AF.Square, accum_out=mstB[:, 0:1]),
                      ["xr%d" % xsl], ["junk", "mstB"])
                S.add("act", lambda e: e.activation(out=mstB[:, 1:2], in_=mstB[:, 0:1], func=AF.Ln, scale=1.0 / D, bias=epsc[:, 0:1]),
                      ["mstB", "epsc"], ["mstB"])
                S.add("act", lambda e: e.activation(out=mstB[:, 2:3], in_=mstB[:, 1:2], func=AF.Exp, scale=-0.5), ["mstB"], ["mstB"])
                osl = oc[0] % 2
                oc[0] += 1
                S.add("dve", lambda e, osl=osl, xsl=xsl: e.scalar_tensor_tensor(out=stg[osl][:], in0=xr[xsl][:], scalar=mstB[:, 2:3], in1=gFb,
                                                                               op0=ALU.mult, op1=ALU.mult),
                      ["xr%d" % xsl, "mstB", "gFb"], ["stg%d" % osl])
                dma(out[j * 128:(j + 1) * 128, :], stg[osl][:], ["stg%d" % osl], [], "dout%d" % osl)

        if stages >= 5:
            f0(0)
            for i_ in range(NST):
                f1(i_)

        S.assign()
        dh = {n: es.enter_context(nc.semaphore("d_" + n)) for n in sorted(dnames)}
        with nc.Block() as block:
            @block.sync
            def _(eng):
                S.emit_engine("sp", eng, sems, dh, final_waits=[n for n in sorted(dnames)])

            @block.scalar
            def _(eng):
                S.emit_engine("act", eng, sems, dh)

            @block.vector
            def _(eng):
                S.emit_engine("dve", eng, sems, dh)

            @block.gpsimd
            def _(eng):
                S.emit_engine("pool", eng, sems, dh)

            @block.tensor
            def _(eng):
                S.emit_engine("pe", eng, sems, dh)
    return nc


def _consts():
    c = np.zeros((128, NCONST), np.float32)
    s = np.arange(128)[:, None]
    t = np.arange(128)[None, :]
    c[:, C_ID:C_ID + 128] = np.eye(128, dtype=np.float32)
    c[:, C_ONE:C_ONE + 128] = 1.0
    n16 = np.float32(-1.0 / 16.0)
    c[:, C_UN:C_UN + 128] = np.where(s <= t, n16, 0)
    c[:, C_UTN:C_UTN + 128] = np.where(s >= t, n16, 0)
    c[:, C_LSN:C_LSN + 128] = np.where(s > t, n16, 0)
    c[:, C_USN:C_USN + 128] = np.where(s < t, n16, 0)
    mf = np.where(s <= t, 1.0, 0.0).astype(np.float32)
    mb = np.where(s >= t, 1.0, 0.0).astype(np.float32)
    c[:, C_MF:C_MF + 512] = np.tile(mf, (1, 4))
    c[:, C_MB:C_MB + 512] = np.tile(mb, (1, 4))
    return c


def _fmajor(w):
    return np.ascontiguousarray(w.reshape(8, 128, 22, 128).transpose(2, 1, 0, 3).reshape(22 * 128, 8 * 128))


def _core_inputs(inp, b, half, T):
    f32 = lambda a: np.ascontiguousarray(np.asarray(a, dtype=np.float32))
    x = np.asarray(inp["x"], dtype=np.float32)
    w_in = np.asarray(inp["w_in"], dtype=np.float32)[0]
    q, k, v, g = w_in[:, 0:256], w_in[:, 256:512], w_in[:, 512:1024], w_in[:, 1024:1536]
    lrf, lrb = w_in[:, 1536:1552], w_in[:, 1552:1568]
    u, gvv = w_in[:, 1568:2080], w_in[:, 2080:2592]
    wdf, bdf = np.asarray(inp["w_decay_f"], np.float32)[0], np.asarray(inp["b_decay_f"], np.float32)[0]
    wdb, bdb = np.asarray(inp["w_decay_b"], np.float32)[0], np.asarray(inp["b_decay_b"], np.float32)[0]
    ws = np.asarray(inp["w_spatial"], np.float32)[0]
    bs = np.asarray(inp["b_spatial"], np.float32)[0]
    if half == 1:
        x_own, x_oth = x[b, T:2 * T], x[b, 0:T]
        LF, LB, WF, BF_, WB, BB = lrf, lrb, wdf, bdf, wdb, bdb
    else:
        x_own, x_oth = x[b, 0:T][::-1], x[b, T:2 * T][::-1]
        LF, LB, WF, BF_, WB, BB = lrb, lrf, wdb, bdb, wdf, bdf
        ws = ws[:, ::-1, ::-1]
        bs = bs[:, ::-1]
    w_in_dev = np.concatenate([q, k, g, u, k, LF, LB, v, gvv], axis=1)
    assert w_in_dev.shape == (D, NIN)
    wdblk = np.zeros((33, 512), np.float32)
    wdblk[0:16, 0:256] = WF
    wdblk[16:32, 256:512] = WB
    wdblk[32, 0:256] = BF_
    wdblk[32, 256:512] = BB
    pvec = np.zeros((128, 24), np.float32)
    pvec[:, 0:8] = np.asarray(inp["norm1_g"], np.float32)[0].reshape(8, 128).T
    pvec[:, 8:16] = np.asarray(inp["norm2_g"], np.float32)[0].reshape(8, 128).T
    pvec[:, 16:20] = np.asarray(inp["gla_norm_g"], np.float32)[0].reshape(4, 128).T
    pvec[:, 20:24] = np.asarray(inp["gmlp_ln_g"], np.float32)[0].reshape(4, 128).T
    rvec = np.zeros((1, 1024), np.float32)
    rvec[0, 0:512] = np.asarray(inp["gmlp_ln_b"], np.float32)[0]
    rvec[0, 512:1024] = bs.reshape(512)
    wsT = np.transpose(ws, (2, 0, 1)).reshape(128, 512)
    return {
        "x_own": f32(x_own), "x_oth": f32(x_oth), "w_in": f32(w_in_dev), "wdblk": wdblk,
        "consts": _consts(), "pvec": pvec, "rvec": rvec, "gF": f32(inp["final_norm_g"]),
        "wsT": f32(wsT), "w_out": f32(np.asarray(inp["w_out"], np.float32)[0]),
        "w_gate": f32(_fmajor(np.asarray(inp["w_gate"], np.float32)[0])), "w_up": f32(_fmajor(np.asarray(inp["w_up"], np.float32)[0])),
        "w_down": f32(np.asarray(inp["w_down"], np.float32)[0]),
    }


def kernel(**inputs):
    x = np.asarray(inputs["x"])
    B, SEQ, _ = x.shape
    T = SEQ // 2
    NT = T // 128
    ncores = 2 * B
    nc = build_nc(NT)
    in_maps = [_core_inputs(inputs, c // 2, c % 2, T) for c in range(ncores)]
    res = run_bass_kernel_spmd(nc, in_maps, core_ids=list(range(ncores)))
    outp = np.empty((B, SEQ, D), np.float32)
    for c in range(ncores):
        o = np.asarray(res.results[c]["out"], dtype=np.float32)
        b, half = c // 2, c % 2
        if half == 1:
            outp[b, T:2 * T] = o
        else:
            outp[b, 0:T] = o[::-1]
    return outp
```

```python
import numpy as np
from contextlib import ExitStack
import concourse.bass as bass
import concourse.mybir as mybir
from concourse.bass_utils import run_bass_kernel_spmd

F32 = mybir.dt.float32
BF16 = mybir.dt.bfloat16
AF = mybir.ActivationFunctionType
ALU = mybir.AluOpType

D = 1024
DFF = 2816
NFM = 1536
NTM = 1312
NIN = NFM + NTM
EPS = 1e-6
C_ID, C_ONE, C_UN, C_UTN, C_LSN, C_USN, C_MF, C_MB = 0, 128, 256, 384, 512, 640, 768, 1280
NCONST = 1792


class _Op:
    __slots__ = ("eng", "fn", "deps", "ddeps", "signal", "val", "dsem")


class Sched:
    ENG = ("pe", "act", "dve", "pool", "sp")

    def __init__(self):
        self.ops = {e: [] for e in self.ENG}
        self.tok = {}
        self.dsems = {}

    PSUM_TOK = ("PT", "PA", "PB", "PC", "PD", "PE", "PF", "PG")

    def add(self, eng, fn, reads=(), writes=(), dsem=None):
        pr = [t for t in reads if t in self.PSUM_TOK]
        if pr:
            reads = [t for t in reads if t not in self.PSUM_TOK]
            writes = list(writes) + pr
        op = _Op()
        op.eng, op.fn, op.signal, op.val, op.dsem = eng, fn, False, 0, dsem
        deps = set()
        for t in reads:
            st = self.tok.setdefault(t, [None, []])
            if st[0] is not None:
                deps.add(st[0])
        for t in writes:
            st = self.tok.setdefault(t, [None, []])
            if st[0] is not None:
                deps.add(st[0])
            deps.update(st[1])
        for t in reads:
            self.tok[t][1].append(op)
        for t in writes:
            self.tok[t][0] = op
            self.tok[t][1] = []
        deps.discard(op)
        op.deps, op.ddeps = [], {}
        for d in deps:
            if d.dsem is not None:
                op.ddeps[d.dsem] = self.dsems[d.dsem]
            elif d.eng == eng and eng == "pe":
                continue
            else:
                op.deps.append(d)
                d.signal = True
        if dsem is not None:
            self.dsems[dsem] = self.dsems.get(dsem, 0) + 16
        self.ops[eng].append(op)
        return op

    def assign(self):
        for e in self.ENG:
            c = 0
            for op in self.ops[e]:
                if op.dsem is None and op.signal:
                    c += 1
                    op.val = c

    def emit_engine(self, e, eng, sems, dh, final_waits=()):
        waited = {}
        for op in self.ops[e]:
            need = {}
            for d in op.deps:
                key = ("e", d.eng)
                if d.val > need.get(key, 0):
                    need[key] = d.val
            for k, v in op.ddeps.items():
                need[("d", k)] = v
            for key, v in need.items():
                if waited.get(key, 0) >= v:
                    continue
                waited[key] = v
                h = dh[key[1]] if key[0] == "d" else sems[key[1]]
                eng.wait_ge(h, v)
            ins = op.fn(eng)
            if op.dsem is not None:
                ins.then_inc(dh[op.dsem], 16)
            elif op.signal:
                ins.then_inc(sems[e], 1)
        for name in final_waits:
            eng.wait_ge(dh[name], self.dsems[name])


def _pipeline(n, stages, per_step=None, order=None):
    ns = len(stages)
    for step in range(n + ns - 1):
        for s in (order if order is not None else reversed(range(ns))):
            i = step - s
            if 0 <= i < n:
                stages[s](i)
        if per_step is not None:
            per_step(step, n + ns - 1)


def build_nc(NT=32, stages=9):
    T = NT * 128
    NST = NT // 4
    nc = bass.Bass("TRN2", target_bir_lowering=False)
    dt_in = lambda name, shape: nc.dram_tensor(name, shape, F32, kind="ExternalInput").ap()
    x_own = dt_in("x_own", [T, D])
    x_oth = dt_in("x_oth", [T, D])
    w_in = dt_in("w_in", [D, NIN])
    wdblk = dt_in("wdblk", [33, 512])
    consts = dt_in("consts", [128, NCONST])
    pvec = dt_in("pvec", [128, 24])
    rvec = dt_in("rvec", [1, 1024])
    gF = dt_in("gF", [D])
    wsT = dt_in("wsT", [128, 512])
    w_out = dt_in("w_out", [D, D])
    w_gate = dt_in("w_gate", [DFF, D])
    w_up = dt_in("w_up", [DFF, D])
    w_down = dt_in("w_down", [DFF, D])
    out = nc.dram_tensor("out", [T, D], F32, kind="ExternalOutput").ap()
    x1s = nc.dram_tensor("x1s", [T, D], F32).ap()

    S = Sched()
    dnames = set()

    def dma(outap, inap, reads, writes, dsem, add=None):
        dnames.add(dsem)
        return (add or S.add)("sp", lambda e: e.dma_start(out=outap, in_=inap), reads, writes, dsem)

    with ExitStack() as es:
        sb = lambda name, shape, dt: es.enter_context(nc.sbuf_tensor(name, shape, dt))
        PT = es.enter_context(nc.psum_tensor("PT", [128, 1024], BF16))
        PB = [es.enter_context(nc.psum_tensor("PB%d" % i, [128, 512], F32)) for i in range(7)]
        PA_, PB_, PC_, PD_, PE_, PF_, PG_ = PB
        cstI = sb("cstI", [128, 256], F32)
        idb = sb("idb", [128, 128], BF16)
        oneb = sb("oneb", [128, 128], BF16)
        pv = sb("pv", [128, 24], F32)
        epsc = sb("epsc", [128, 1], F32)
        bsc = sb("bsc", [128, 16], F32)
        stg = [sb("stg%d" % i, [128, 1024], F32) for i in range(3)]
        xs = [sb("xs%d" % i, [128, D], F32) for i in range(2)]
        xr = [sb("xr%d" % i, [128, D], F32) for i in range(2)]
        junk = sb("junk", [128, D], BF16)
        hb = [sb("hb%d" % i, [128, D], BF16) for i in range(2)]
        ss = [sb("ss%d" % i, [128, 2], F32) for i in range(2)]
        rstd = [sb("rstd%d" % i, [128, 1], F32) for i in range(2)]
        xT = sb("xT", [128, 8, 512], BF16)
        ARENA_F32 = 41984
        arena = sb("arena", [128, ARENA_F32], F32)
        cur = [0]

        def carve(ncols_elem, dt):
            nf = ncols_elem if dt == F32 else (ncols_elem + 1) // 2
            a = arena[:, cur[0]:cur[0] + nf]
            cur[0] += nf
            assert cur[0] <= ARENA_F32, cur[0]
            if dt != F32:
                a = a.bitcast(dt)
            return a

        cur[0] = 0
        winb = carve(8 * NIN, BF16).rearrange("p (k c) -> p k c", k=8)
        woutb = carve(8 * D, BF16).rearrange("p (k c) -> p k c", k=8)
        Sbst = carve(NT * 256, BF16).rearrange("p (j q v) -> p j q v", j=NT, q=2)
        cst = carve(NCONST, F32)
        wdb = carve(512, F32)[0:33, :]
        wsb = carve(512, BF16)
        Cg = carve(512, F32)
        lrT = [carve(128, F32)[0:33, :] for _ in range(4)]
        Sst = carve(256, F32).rearrange("p (q v) -> p q v", q=2)
        Sf = carve(256, F32).rearrange("p (q v) -> p q v", q=2)
        Sfb = carve(256, BF16).rearrange("p (q v) -> p q v", q=2)
        decs = [carve(2, F32) for _ in range(2)]
        mst = carve(8, F32)
        union0 = cur[0]
        rv = carve(1024, F32)[0:1, :]
        wsf = carve(512, F32)
        rsrow = carve(512, F32)[0:1, :]
        cur[0] = union0
        NSL = 4
        xTp = [carve(1024, BF16).rearrange("p (k t) -> p k t", k=8) for _ in range(NSL)]
        p_vb = [carve(512, BF16) for _ in range(NSL)]
        p_ktm = [carve(256, F32) for _ in range(NSL)]
        p_lr = [carve(32, F32) for _ in range(NSL)]
        p_lap = [carve(256, F32) for _ in range(NSL)]
        p_Eend = [carve(256, F32) for _ in range(NSL)]
        p_kte = [carve(256, BF16) for _ in range(NSL)]
        cur[0] = union0
        qT = carve(1024, F32).rearrange("p (q t) -> p q t", q=2)
        kT = carve(1024, F32).rearrange("p (q t) -> p q t", q=2)
        sgT = carve(2048, F32).rearrange("p (h t) -> p h t", h=4)
        uT = carve(2048, F32).rearrange("p (h t) -> p h t", h=4)
        vb = carve(4 * 512, BF16).rearrange("p (s c) -> p s c", s=4)
        gv = carve(4 * 512, F32).rearrange("p (s c) -> p s c", s=4)
        ktm = carve(4 * 256, F32).rearrange("p (s c) -> p s c", s=4)
        lrs = carve(4 * 32, F32).rearrange("p (s c) -> p s c", s=4)
        lnst = carve(4 * 2, F32).rearrange("p (s c) -> p s c", s=4)
        yT = carve(8 * 512, BF16).rearrange("p (k t) -> p k t", k=8)
        lap4 = [carve(512, F32) for _ in range(4)]
        lap0 = lap4[0]
        Ee0 = carve(512, F32)
        Ei0 = carve(512, F32)
        qdec0 = carve(512, BF16)
        kdec0 = carve(512, BF16)
        scF0 = carve(512, BF16)
        scB0 = carve(512, BF16)
        oTs0 = carve(512, F32)
        lap1, Ee1 = stg[0][:, 0:512], stg[0][:, 512:1024]
        Ei1 = stg[1][:, 0:512]
        qdec1 = stg[1][:, 512:768].bitcast(BF16)
        kdec1 = stg[1][:, 768:1024].bitcast(BF16)
        scF1 = stg[2][:, 0:256].bitcast(BF16)
        scB1 = stg[2][:, 256:512].bitcast(BF16)
        oTs1 = stg[2][:, 512:1024]
        GB = []
        for (lap_, Ee_, Ei_, qd_, kd_, sF_, sB_, oT_) in ((lap0, Ee0, Ei0, qdec0, kdec0, scF0, scB0, oTs0),
                                                         (lap1, Ee1, Ei1, qdec1, kdec1, scF1, scB1, oTs1)):
            GB.append(dict(lap=lap_, Ee=Ee_, Ei=Ei_, qdec=qd_.rearrange("p (d c) -> p d c", d=2),
                           kdec=kd_.rearrange("p (d c) -> p d c", d=2), scF=sF_, scB=sB_, oTs=oT_,
                           Eend=lap_[:, 0:256], rsb=Ei_, osq=sF_, kte=kd_[:, 0:256]))
        vn = carve(512, BF16)
        tmpg = carve(512, F32)
        endA = cur[0]
        cur[0] = 0
        wgb = carve(8 * DFF, BF16).rearrange("p (k c) -> p k c", k=8)
        wub = carve(8 * DFF, BF16).rearrange("p (k c) -> p k c", k=8)
        wdnb = carve(22 * D, BF16).rearrange("p (f c) -> p f c", f=22)
        actT = carve(22 * 512, BF16).rearrange("p (f t) -> p f t", f=22)
        sgs = [carve(512, F32) for _ in range(2)]
        gFb = carve(D, F32)
        mstB = carve(8, F32)
        endB = cur[0]
        assert max(endA, endB) <= ARENA_F32, (endA, endB)
        print("arena use (f32 cols): phaseA %d phaseB %d of %d" % (endA, endB, ARENA_F32))

        sems = {e: es.enter_context(nc.semaphore("s_" + e)) for e in Sched.ENG}

        bcount = [0]

        def barrier(extra_reads=()):
            n = bcount[0]
            bcount[0] += 1
            tag = "bar%d" % n
            S.add("act", lambda e: e.activation(out=bsc[:, 0:1], in_=epsc[:, 0:1], func=AF.Copy), ["epsc"], [tag + "a"])
            S.add("pool", lambda e: e.memset(bsc[:, 1:2], 0.0), [], [tag + "p"])
            S.add("pe", lambda e: e.matmul(PA_[0:1, 0:1], lhsT=cstI[:, 128:129], rhs=cstI[:, 128:129], start=True, stop=True),
                  ["cstI"], ["PA"])
            S.add("dve", lambda e: e.tensor_copy(out=bsc[0:1, 2:3], in_=PA_[0:1, 0:1]),
                  ["PA", tag + "a", tag + "p"] + list(extra_reads), [tag])
            S.add("act", lambda e: e.activation(out=bsc[:, 3:4], in_=epsc[:, 0:1], func=AF.Copy), ["epsc", tag], [tag + "ra"])
            S.add("pool", lambda e: e.memset(bsc[:, 4:5], 0.0), [tag], [tag + "rp"])
            S.add("pe", lambda e: e.matmul(PA_[0:1, 0:1], lhsT=cstI[:, 128:129], rhs=cstI[:, 128:129], start=True, stop=True),
                  ["cstI", tag], ["PA"])
            return tag

        dma(cstI[:], consts[:, 0:256], [], ["cstI"], "dconst")
        dma(cst[:], consts, [], ["cst"], "dconst")
        dma(pv[:], pvec, [], ["pv"], "dconst")
        dma(rv, rvec, [], ["rv"], "dconst")
        dma(wdb, wdblk, [], ["wdb"], "dconst")
        dma(wsf, wsT, [], ["wsf"], "dconst")
        S.add("dve", lambda e: e.memset(epsc[:], EPS), [], ["epsc"])
        S.add("dve", lambda e: e.tensor_copy(out=idb[:], in_=cstI[:, 0:128]), ["cstI"], ["idb"])
        S.add("dve", lambda e: e.tensor_copy(out=oneb[:], in_=cstI[:, 128:256]), ["cstI"], ["oneb"])
        S.add("dve", lambda e: e.tensor_copy(out=wsb, in_=wsf), ["wsf"], ["wsb"])
        for i in range(4):
            S.add("dve", lambda e, i=i: e.memset(lrT[i][32:33, :], 1.0), [], ["lrT1_%d" % i])
        S.add("dve", lambda e: e.memset(Sst, 0.0), [], ["Sst"])
        S.add("pe", lambda e: e.matmul(PA_[0:1, :], lhsT=cstI[:, 128:129], rhs=wsf, start=True, stop=True),
              ["cstI", "wsf"], ["PA"])
        S.add("dve", lambda e: e.tensor_copy(out=rsrow, in_=PA_[0:1, :]), ["PA"], ["rsrow"])
        for g in range(4):
            S.add("pe", lambda e, g=g: e.matmul(PB_[:, g * 128:(g + 1) * 128], lhsT=rv[0:1, g * 128:(g + 1) * 128],
                                                rhs=rsrow[0:1, g * 128:(g + 1) * 128], start=True, stop=False),
                  ["rv", "rsrow"], ["PB"])
            S.add("pe", lambda e, g=g: e.matmul(PB_[:, g * 128:(g + 1) * 128], lhsT=cstI[0:1, 128:256],
                                                rhs=rv[0:1, 512 + g * 128:512 + (g + 1) * 128], start=False, stop=True),
                  ["rv", "cstI"], ["PB"])
        S.add("dve", lambda e: e.tensor_copy(out=Cg, in_=PB_[:, :]), ["PB"], ["Cg"])

        wl = [0]

        def weight_chunks(dst3, src2, ktiles, col_ranges, scale_col0=None, tokname="w", extra_reads=()):
            th = []
            for k in range(ktiles):
                for (lo, hi) in col_ranges:
                    c0 = lo
                    while c0 < hi:
                        cw = min(1024, hi - c0)

                        def thunk(k=k, c0=c0, cw=cw):
                            sl = wl[0] % 3
                            wl[0] += 1
                            use_act = (wl[0] % 2) == 0
                            dma(stg[sl][:, 0:cw], src2[k * 128:(k + 1) * 128, c0:c0 + cw], list(extra_reads), ["stg%d" % sl], "dstg%d" % sl)
                            rd = ["stg%d" % sl] + list(extra_reads)
                            if scale_col0 is None:
                                if use_act:
                                    S.add("act", lambda e: e.activation(out=dst3[:, k, c0:c0 + cw], in_=stg[sl][:, 0:cw], func=AF.Copy), rd, [tokname])
                                else:
                                    S.add("dve", lambda e: e.tensor_copy(out=dst3[:, k, c0:c0 + cw], in_=stg[sl][:, 0:cw]), rd, [tokname])
                            else:
                                sc = pv[:, scale_col0 + k:scale_col0 + k + 1]
                                if use_act:
                                    S.add("act", lambda e: e.activation(out=dst3[:, k, c0:c0 + cw], in_=stg[sl][:, 0:cw], func=AF.Copy, scale=sc),
                                          rd + ["pv"], [tokname])
                                else:
                                    S.add("dve", lambda e: e.tensor_scalar(out=dst3[:, k, c0:c0 + cw], in0=stg[sl][:, 0:cw], scalar1=sc,
                                                                           scalar2=None, op0=ALU.mult), rd + ["pv"], [tokname])
                        th.append(thunk)
                        c0 += cw
            return th

        barrier()
        for t_ in weight_chunks(winb, w_in, 8, [(NFM, NFM + 800)], tokname="winb_tm"):
            t_()
        late_w = (weight_chunks(winb, w_in, 8, [(0, NFM), (NFM + 800, NIN)], tokname="winb_rest")
                  + weight_chunks(woutb, w_out, 8, [(0, D)], tokname="woutb"))

        def late_hook(step, nsteps):
            want = (step + 1) * len(late_w) // max(1, nsteps - 6)
            while late_hook.i < min(want, len(late_w)):
                late_w[late_hook.i]()
                late_hook.i += 1
        late_hook.i = 0

        pc = [0]

        def prep_load(src_rows, src_reads=(), add=None):
            sl = pc[0] % 2
            pc[0] += 1
            dma(xs[sl][:], src_rows, list(src_reads), ["xs%d" % sl], "dxs%d" % sl, add=add)
            add = add or S.add
            add("act", lambda e: e.activation(out=junk[:], in_=xs[sl][:], func=AF.Square, accum_out=ss[sl][:, 0:1]),
                  ["xs%d" % sl], ["junk", "ss%d" % sl])
            add("act", lambda e: e.activation(out=ss[sl][:, 1:2], in_=ss[sl][:, 0:1], func=AF.Ln, scale=1.0 / D, bias=epsc[:, 0:1]),
                  ["ss%d" % sl, "epsc"], ["ss%d" % sl])
            add("act", lambda e: e.activation(out=rstd[sl][:], in_=ss[sl][:, 1:2], func=AF.Exp, scale=-0.5),
                  ["ss%d" % sl], ["rstd%d" % sl])
            add("act", lambda e: e.activation(out=hb[sl][:], in_=xs[sl][:], func=AF.Copy, scale=rstd[sl][:, 0:1]),
                  ["xs%d" % sl, "rstd%d" % sl], ["hb%d" % sl])
            return sl

        def prep_transpose(sl, dst3, dtok, gcol=0, add=None):
            add = add or S.add
            for k in range(8):
                add("pe", lambda e, k=k: e.transpose(out=PT[:, k * 128:(k + 1) * 128], in_=hb[sl][:, k * 128:(k + 1) * 128], identity=idb[:]),
                      ["hb%d" % sl, "idb"], ["PT"])
            add("dve", lambda e: e.tensor_tensor(out=dst3, in0=PT[:].rearrange("p (k t) -> p k t", k=8),
                                                   in1=pv[:, gcol:gcol + 8].unsqueeze(2).to_broadcast([128, 8, 128]), op=ALU.mult),
                  ["PT", "pv"], [dtok])

        def la_from_lr(lr_ap, lr_tok, slot, zcols, lap_ap, lap_tok, zbank, zbank_tok, add=None):
            add = add or S.add
            add("pe", lambda e: e.transpose(out=zbank[0:32, 0:128], in_=lr_ap, identity=cstI[:, 0:128]),
                [lr_tok, "cstI"], [zbank_tok])
            add("dve", lambda e: e.tensor_copy(out=lrT[slot][0:32, :], in_=zbank[0:32, 0:128]), [zbank_tok], ["lrT%d" % slot])
            c0, c1 = zcols
            add("pe", lambda e: e.matmul(zbank[:, c0:c1], lhsT=lrT[slot][:, :], rhs=wdb[:, c0:c1], start=True, stop=True),
                ["lrT%d" % slot, "lrT1_%d" % slot, "wdb"], [zbank_tok])
            add("act", lambda e: e.activation(out=lap_ap, in_=zbank[:, c0:c1], func=AF.Exp, scale=-1.0), [zbank_tok], [lap_tok])
            add("act", lambda e: e.activation(out=lap_ap, in_=lap_ap, func=AF.Ln, bias=1.0), [lap_tok], [lap_tok])

        def state_update(Sm, Sm_tok, dps, dps_tok, dec_ap, dec_tok, add=None):
            add = add or S.add
            for p in range(2):
                for r in range(2):
                    add("dve", lambda e, p=p, r=r: e.scalar_tensor_tensor(
                        out=Sm[r * 64:(r + 1) * 64, p, :], in0=Sm[r * 64:(r + 1) * 64, p, :],
                        scalar=dec_ap[r * 64:(r + 1) * 64, p:p + 1],
                        in1=dps[r * 64:(r + 1) * 64, p * 256 + r * 128:p * 256 + (r + 1) * 128],
                        op0=ALU.mult, op1=ALU.add), [Sm_tok, dps_tok, dec_tok], [Sm_tok])

        def scan_items(src, order, dirn, store, Sm, Sm_tok):
            zc = (dirn * 256, dirn * 256 + 256)
            tmat = C_LSN if dirn == 0 else C_USN
            return [(src, j, zc, tmat, store, Sm, Sm_tok) for j in order]

        items = []
        if stages >= 1:
            it1 = scan_items(x_oth, list(range(NT)), 0, False, Sf, "Sf")
            it2 = scan_items(x_own, list(range(NT - 1, -1, -1)), 1, True, Sst, "Sst") if stages >= 2 else []
            for a_, b_ in zip(it1, it2 if it2 else [None] * len(it1)):
                items.append(a_)
                if b_ is not None:
                    items.append(b_)
        S.add("dve", lambda e: e.memset(Sf, 0.0), [], ["Sf"])
        sls = {}

        def st0(i):
            src, j = items[i][0], items[i][1]
            sls[i] = prep_load(src[j * 128:(j + 1) * 128, :])

        def st1a(i):
            s = i % NSL
            prep_transpose(sls[i], xTp[s], "xTp%d" % s)

        def st1b(i):
            s = i % NSL
            for k in range(8):
                S.add("pe", lambda e, k=k: e.matmul(PA_[:, 0:288], lhsT=xTp[s][:, k, :], rhs=winb[:, k, NFM:NFM + 288],
                                                    start=(k == 0), stop=(k == 7)), ["xTp%d" % s, "winb_tm"], ["PA"])
            for k in range(8):
                S.add("pe", lambda e, k=k: e.matmul(PB_[:, :], lhsT=xTp[s][:, k, :], rhs=winb[:, k, NFM + 288:NFM + 800],
                                                    start=(k == 0), stop=(k == 7)), ["xTp%d" % s, "winb_tm"], ["PB"])
            S.add("act", lambda e: e.activation(out=p_ktm[s], in_=PA_[:, 0:256], func=AF.Copy), ["PA"], ["p_ktm%d" % s])
            S.add("act", lambda e: e.activation(out=p_lr[s], in_=PA_[:, 256:288], func=AF.Copy), ["PA"], ["p_lr%d" % s])
            S.add("dve", lambda e: e.tensor_copy(out=p_vb[s], in_=PB_[:, :]), ["PB"], ["p_vb%d" % s])

        def st1c1(i):
            s = i % NSL
            q = i % 2
            S.add("pe", lambda e: e.transpose(out=PC_[0:32, 0:128], in_=p_lr[s], identity=cstI[:, 0:128]), ["p_lr%d" % s, "cstI"], ["PC"])
            S.add("dve", lambda e: e.tensor_copy(out=lrT[q][0:32, :], in_=PC_[0:32, 0:128]), ["PC"], ["lrT%d" % q])

        def st1c2(i):
            s = i % NSL
            q = i % 2
            c0, c1 = items[i][2]
            S.add("pe", lambda e: e.matmul(PC_[:, c0:c1], lhsT=lrT[q][:, :], rhs=wdb[:, c0:c1], start=True, stop=True),
                  ["lrT%d" % q, "lrT1_%d" % q, "wdb"], ["PC"])
            S.add("act", lambda e: e.activation(out=p_lap[s], in_=PC_[:, c0:c1], func=AF.Exp, scale=-1.0), ["PC"], ["p_lap%d" % s])
            S.add("act", lambda e: e.activation(out=p_lap[s], in_=p_lap[s], func=AF.Ln, bias=1.0), ["p_lap%d" % s], ["p_lap%d" % s])

        def st2a(i):
            s = i % NSL
            d = i % 2
            src, j, zc, tmat, store, Sm, Sm_tok = items[i]
            S.add("pe", lambda e: e.matmul(PD_[:, 0:256], lhsT=cst[:, tmat:tmat + 128], rhs=p_lap[s], start=True, stop=True),
                  ["cst", "p_lap%d" % s], ["PD"])
            for p in range(2):
                S.add("pe", lambda e, p=p: e.matmul(PD_[:, 256 + p:257 + p], lhsT=p_lap[s][:, p * 128:(p + 1) * 128],
                                                    rhs=cst[:, C_UN + 127:C_UN + 128], start=True, stop=True),
                      ["cst", "p_lap%d" % s], ["PD"])
            S.add("act", lambda e: e.activation(out=p_Eend[s], in_=PD_[:, 0:256], func=AF.Exp), ["PD"], ["p_Eend%d" % s])
            S.add("act", lambda e: e.activation(out=decs[d], in_=PD_[:, 256:258], func=AF.Exp), ["PD"], ["decs%d" % d])
            S.add("dve", lambda e: e.tensor_tensor(out=p_kte[s], in0=p_ktm[s], in1=p_Eend[s], op=ALU.mult),
                  ["p_ktm%d" % s, "p_Eend%d" % s], ["p_kte%d" % s])

        def st2b(i):
            s = i % NSL
            d = i % 2
            src, j, zc, tmat, store, Sm, Sm_tok = items[i]
            for p in range(2):
                S.add("pe", lambda e, p=p: e.matmul(PE_[:, p * 256:(p + 1) * 256], lhsT=p_kte[s][:, p * 128:(p + 1) * 128],
                                                    rhs=p_vb[s][:, p * 256:(p + 1) * 256], start=True, stop=True),
                      ["p_kte%d" % s, "p_vb%d" % s], ["PE"])
            if store:
                S.add("act", lambda e: e.activation(out=Sbst[:, j, :, :], in_=Sm, func=AF.Copy), [Sm_tok], ["Sbst%d" % j])
            state_update(Sm, Sm_tok, PE_, "PE", decs[d], "decs%d" % d)

        sched_ = [(st1c1, 3), (st2b, 5), (st2a, 4), (st1c2, 3), (st1b, 2), (st1a, 1), (st0, 0)]
        nst_ = len(items) + 5
        for step_ in range(nst_):
            for fn_, lag_ in sched_:
                i_ = step_ - lag_
                if 0 <= i_ < len(items):
                    fn_(i_)
            late_hook(step_, nst_)
        while late_hook.i < len(late_w):
            late_w[late_hook.i]()
            late_hook.i += 1
        S.add("act", lambda e: e.activation(out=Sfb, in_=Sf, func=AF.Copy), ["Sf"], ["Sfb"])
        barrier()

        xrc = [0]
        sub_sl = {}

        def m0(i):
            for sub in range(4):
                j = i * 4 + sub
                sub_sl[j] = prep_load(x_own[j * 128:(j + 1) * 128, :])
                prep_transpose(sub_sl[j], xT[:, :, sub * 128:(sub + 1) * 128], "xT")

        def m1(i):
            fm_banks = [(PF_, "PF"), (PG_, "PG")]
            for c in range(12):
                bank, btok = fm_banks[c % 2]
                for k in range(8):
                    S.add("pe", lambda e, c=c, k=k, bank=bank: e.matmul(bank[:, :], lhsT=winb[:, k, c * 128:(c + 1) * 128], rhs=xT[:, k, :],
                                                                       start=(k == 0), stop=(k == 7)), ["winb_tm", "winb_rest", "xT"], [btok])
                if c < 2:
                    S.add("act", lambda e, c=c, bank=bank: e.activation(out=qT[:, c, :], in_=bank[:, :], func=AF.Copy), [btok], ["qT"])
                elif c < 4:
                    S.add("act", lambda e, c=c, bank=bank: e.activation(out=kT[:, c - 2, :], in_=bank[:, :], func=AF.Copy), [btok], ["kT"])
                elif c < 8:
                    S.add("act", lambda e, c=c, bank=bank: e.activation(out=sgT[:, c - 4, :], in_=bank[:, :], func=AF.Silu), [btok], ["sgT"])
                else:
                    S.add("act", lambda e, c=c, bank=bank: e.activation(out=uT[:, c - 8, :], in_=bank[:, :], func=AF.Gelu), [btok], ["uT"])
            for sub in range(4):
                tsl = slice(sub * 128, (sub + 1) * 128)
                for (bank, btok, c0, cw) in ((PA_, "PA", 0, 288), (PB_, "PB", 288, 512), (PC_, "PC", 800, 512)):
                    for k in range(8):
                        S.add("pe", lambda e, k=k, bank=bank, c0=c0, cw=cw, tsl=tsl: e.matmul(
                            bank[:, 0:cw], lhsT=xT[:, k, tsl], rhs=winb[:, k, NFM + c0:NFM + c0 + cw],
                            start=(k == 0), stop=(k == 7)), ["winb_tm", "winb_rest", "xT"], [btok])
                S.add("dve", lambda e, sub=sub: e.tensor_copy(out=ktm[:, sub, :], in_=PA_[:, 0:256]), ["PA"], ["ktm%d" % sub])
                S.add("dve", lambda e, sub=sub: e.tensor_copy(out=lrs[:, sub, :], in_=PA_[:, 256:288]), ["PA"], ["lrs%d" % sub])
                S.add("dve", lambda e, sub=sub: e.tensor_copy(out=vb[:, sub, :], in_=PB_[:, :]), ["PB"], ["vb%d" % sub])
                S.add("act", lambda e, sub=sub: e.activation(out=gv[:, sub, :], in_=PC_[:, :], func=AF.Gelu,
                                                             accum_out=lnst[:, sub, 0:1]), ["PC"], ["gv%d" % sub, "lnst%d" % sub])
            las = []
            for sub in range(4):
                l_ = []
                la_from_lr(lrs[:, sub, :], "lrs%d" % sub, sub, (0, 512), lap4[sub], "lap4_%d" % sub,
                           (PA_, PB_, PC_, PD_)[sub], ("PA", "PB", "PC", "PD")[sub],
                           add=lambda *a, l_=l_, **k: l_.append(lambda: S.add(*a, **k)))
                las.append(l_)
            for k_ in range(len(las[0])):
                for sub in range(4):
                    las[sub][k_]()
            glas, gms, ops_ = [], [], []
            for sub in range(4):
                g_, m_, o_ = [], [], []
                gla_sub(i, sub, lambda *a, g_=g_, **k: g_.append(lambda: S.add(*a, **k)))
                gmlp_sub(i, sub, lambda *a, m_=m_, **k: m_.append(lambda: S.add(*a, **k)))
                if sub > 0:
                    outproj_sub(i, sub - 1, lambda *a, o_=o_, **k: o_.append(lambda: S.add(*a, **k)))
                if i + 1 < NST:
                    jn = (i + 1) * 4 + sub
                    A_ = lambda *a, o_=o_, **k: o_.append(lambda: S.add(*a, **k))
                    sln = prep_load(x_own[jn * 128:(jn + 1) * 128, :], add=A_)
                    prep_transpose(sln, xT[:, :, sub * 128:(sub + 1) * 128], "xT", add=A_)
                glas.append(g_)
                gms.append(m_)
                ops_.append(o_)
            L = len(glas[0])
            H = (L + 1) // 2
            mi = [0] * 4
            oi = [0] * 4
            for tick in range(H * 3 + L):
                for sub in range(4):
                    k = tick - sub * H
                    if 0 <= k < L:
                        glas[sub][k]()
                        if k < H:
                            want = (k + 1) * len(gms[sub]) // H
                            while mi[sub] < want:
                                gms[sub][mi[sub]]()
                                mi[sub] += 1
                        else:
                            want = (k + 1 - H) * len(ops_[sub]) // (L - H)
                            while oi[sub] < want:
                                ops_[sub][oi[sub]]()
                                oi[sub] += 1
            for sub in range(4):
                assert mi[sub] == len(gms[sub]) and oi[sub] == len(ops_[sub])
            outproj_sub(i, 3, S.add)

        def gmlp_sub(i, sub, A):
            tsl = slice(sub * 128, (sub + 1) * 128)
            A("act", lambda e: e.activation(out=vn, in_=gv[:, sub, :], func=AF.Square,
                                            accum_out=lnst[:, sub, 1:2]), ["gv%d" % sub], ["vn", "lnst%d" % sub])
            A("dve", lambda e: e.tensor_scalar(out=mst[:, 0:2], in0=lnst[:, sub, :], scalar1=1.0 / 512, scalar2=None, op0=ALU.mult),
              ["lnst%d" % sub], ["mst"])
            A("dve", lambda e: e.tensor_tensor(out=mst[:, 2:3], in0=mst[:, 0:1], in1=mst[:, 0:1], op=ALU.mult), ["mst"], ["mst"])
            A("dve", lambda e: e.tensor_tensor(out=mst[:, 3:4], in0=mst[:, 1:2], in1=mst[:, 2:3], op=ALU.subtract), ["mst"], ["mst"])
            A("act", lambda e: e.activation(out=mst[:, 4:5], in_=mst[:, 3:4], func=AF.Ln, bias=epsc[:, 0:1]), ["mst", "epsc"], ["mst"])
            A("act", lambda e: e.activation(out=mst[:, 5:6], in_=mst[:, 4:5], func=AF.Exp, scale=-0.5), ["mst"], ["mst"])
            A("dve", lambda e: e.tensor_scalar(out=vn, in0=gv[:, sub, :], scalar1=mst[:, 0:1], scalar2=mst[:, 5:6],
                                               op0=ALU.subtract, op1=ALU.mult), ["gv%d" % sub, "mst"], ["vn"])
            for g in range(4):
                A("pe", lambda e, g=g: e.matmul(PF_[:, g * 128:(g + 1) * 128], lhsT=vn[:, g * 128:(g + 1) * 128],
                                                rhs=wsb[:, g * 128:(g + 1) * 128], start=True, stop=True), ["vn", "wsb"], ["PF"])
            for g in range(4):
                A("dve", lambda e, g=g: e.scalar_tensor_tensor(
                    out=tmpg[:, g * 128:(g + 1) * 128], in0=PF_[:, g * 128:(g + 1) * 128], scalar=pv[:, 20 + g:21 + g],
                    in1=Cg[:, g * 128:(g + 1) * 128], op0=ALU.mult, op1=ALU.add), ["PF", "pv", "Cg"], ["tmpg"])
            A("dve", lambda e: e.tensor_tensor(out=yT[:, 4:8, tsl], in0=tmpg.rearrange("p (g t) -> p g t", g=4),
                                               in1=uT[:, :, tsl], op=ALU.mult), ["tmpg", "uT"], ["yTb%d" % sub])

        def gla_sub(i, sub, A):
            j = i * 4 + sub
            q = sub % 2
            B_ = GB[q]
            lap, Ee, Ei, qdec, kdec, scF, scB, oTs = (B_[n] for n in ("lap", "Ee", "Ei", "qdec", "kdec", "scF", "scB", "oTs"))
            Eend, rsb, osq, kte = B_["Eend"], B_["rsb"], B_["osq"], B_["kte"]
            T = lambda n: "%s_%d" % (n, q)
            tsl = slice(sub * 128, (sub + 1) * 128)
            lap = lap4[sub]
            Eend = lap[:, 0:256]
            for dirn in range(2):
                um = C_UN if dirn == 0 else C_UTN
                for p in range(2):
                    A("pe", lambda e, dirn=dirn, p=p, um=um: e.matmul(
                        PB_[:, (dirn * 2 + p) * 128:(dirn * 2 + p + 1) * 128], lhsT=lap[:, dirn * 256 + p * 128:dirn * 256 + (p + 1) * 128],
                        rhs=cst[:, um:um + 128], start=True, stop=True), ["lap4_%d" % sub, "cst"], ["PB"])
            A("act", lambda e: e.activation(out=Ee, in_=PB_[:, :], func=AF.Exp), ["PB"], [T("Ee")])
            A("act", lambda e: e.activation(out=Ei, in_=PB_[:, :], func=AF.Exp, scale=-1.0), ["PB"], [T("Ei")])
            for dirn in range(2):
                A("dve", lambda e, dirn=dirn: e.scalar_tensor_tensor(
                    out=qdec[:, dirn, :].rearrange("p (q t) -> p q t", q=2), in0=qT[:, :, tsl], scalar=0.125,
                    in1=Ee[:, dirn * 256:(dirn + 1) * 256].rearrange("p (q t) -> p q t", q=2), op0=ALU.mult, op1=ALU.mult),
                    ["qT", T("Ee")], [T("qdec")])
                A("dve", lambda e, dirn=dirn: e.tensor_tensor(
                    out=kdec[:, dirn, :].rearrange("p (q t) -> p q t", q=2), in0=kT[:, :, tsl],
                    in1=Ei[:, dirn * 256:(dirn + 1) * 256].rearrange("p (q t) -> p q t", q=2), op=ALU.mult), ["kT", T("Ei")], [T("kdec")])
            for dirn in range(2):
                for p in range(2):
                    for r, (bank, btok) in enumerate(((PC_, "PC"), (PD_, "PD"))):
                        A("pe", lambda e, dirn=dirn, p=p, r=r, bank=bank: e.matmul(
                            bank[:, (dirn * 2 + p) * 128:(dirn * 2 + p + 1) * 128], lhsT=kdec[r * 64:(r + 1) * 64, dirn, p * 128:(p + 1) * 128],
                            rhs=qdec[r * 64:(r + 1) * 64, dirn, p * 128:(p + 1) * 128], start=True, stop=True),
                            [T("kdec"), T("qdec")], [btok])
            A("dve", lambda e: e.tensor_tensor(out=scF, in0=PC_[:, :], in1=cst[:, C_MF + 256:C_MF + 768], op=ALU.mult), ["PC", "cst"], [T("scF")])
            A("dve", lambda e: e.tensor_tensor(out=scB, in0=PD_[:, :], in1=cst[:, C_MF + 256:C_MF + 768], op=ALU.mult), ["PD", "cst"], [T("scB")])
            A("pe", lambda e: e.matmul(PA_[:, 0:256], lhsT=cst[:, C_LSN:C_LSN + 128], rhs=lap[:, 0:256], start=True, stop=True),
              ["cst", "lap4_%d" % sub], ["PA"])
            A("act", lambda e: e.activation(out=Eend, in_=PA_[:, 0:256], func=AF.Exp), ["PA"], ["lap4_%d" % sub])
            A("dve", lambda e: e.tensor_tensor(out=kte, in0=ktm[:, sub, :], in1=Eend, op=ALU.mult), ["ktm%d" % sub, "lap4_%d" % sub], [T("kdec")])
            for h in range(4):
                p, r = h // 2, h % 2
                osl = slice(h * 128, (h + 1) * 128)
                scr = scF if r == 0 else scB
                A("pe", lambda e, p=p, scr=scr, osl=osl: e.matmul(PE_[:, osl], lhsT=vb[:, sub, osl], rhs=scr[:, p * 128:(p + 1) * 128],
                                                                  start=True, stop=False), ["vb%d" % sub, T("scF"), T("scB")], ["PE"])
                A("pe", lambda e, p=p, scr=scr, osl=osl: e.matmul(PE_[:, osl], lhsT=vb[:, sub, osl], rhs=scr[:, (2 + p) * 128:(3 + p) * 128],
                                                                  start=False, stop=False), ["vb%d" % sub, T("scF"), T("scB")], ["PE"])
                A("pe", lambda e, p=p, r=r, osl=osl: e.matmul(PE_[:, osl], lhsT=Sfb[r * 64:(r + 1) * 64, p, :],
                                                              rhs=qdec[r * 64:(r + 1) * 64, 0, p * 128:(p + 1) * 128], start=False, stop=False),
                  ["Sfb", T("qdec")], ["PE"])
                A("pe", lambda e, p=p, r=r, osl=osl: e.matmul(PE_[:, osl], lhsT=Sbst[r * 64:(r + 1) * 64, j, p, :],
                                                              rhs=qdec[r * 64:(r + 1) * 64, 1, p * 128:(p + 1) * 128], start=False, stop=True),
                  ["Sbst%d" % j, T("qdec")], ["PE"])
            A("dve", lambda e: e.tensor_copy(out=oTs, in_=PE_[:, :]), ["PE"], [T("oTs")])
            A("act", lambda e: e.activation(out=osq, in_=oTs, func=AF.Square), [T("oTs")], [T("scF")])
            A("pe", lambda e: e.matmul(PE_[:, :], lhsT=oneb[:], rhs=osq, start=True, stop=True), ["oneb", T("scF")], ["PE"])
            A("act", lambda e: e.activation(out=rsb, in_=PE_[:, :], func=AF.Ln, scale=1.0 / 128, bias=epsc[:, 0:1]), ["PE", "epsc"], [T("Ei")])
            for p in range(2):
                A("pe", lambda e, p=p: e.matmul(PE_[:, p * 256:(p + 1) * 256], lhsT=kte[:, p * 128:(p + 1) * 128],
                                                rhs=vb[:, sub, p * 256:(p + 1) * 256], start=True, stop=True), [T("kdec"), "vb%d" % sub], ["PE"])
            A("act", lambda e: e.activation(out=rsb, in_=rsb, func=AF.Exp, scale=-0.5), [T("Ei")], [T("Ei")])
            A("dve", lambda e: e.tensor_copy(out=decs[q], in_=Ee[:, 127:256:128]), [T("Ee")], ["decs%d" % q])
            state_update(Sf, "Sf", PE_, "PE", decs[q], "decs%d" % q, add=A)
            A("act", lambda e: e.activation(out=Sfb, in_=Sf, func=AF.Copy), ["Sf"], ["Sfb"])
            A("dve", lambda e: e.tensor_tensor(out=oTs, in0=oTs, in1=rsb, op=ALU.mult), [T("oTs"), T("Ei")], [T("oTs")])
            for h in range(4):
                A("dve", lambda e, h=h: e.scalar_tensor_tensor(
                    out=yT[:, h, tsl], in0=oTs[:, h * 128:(h + 1) * 128], scalar=pv[:, 16 + h:17 + h],
                    in1=sgT[:, h, tsl], op0=ALU.mult, op1=ALU.mult), [T("oTs"), "pv", "sgT"], ["yTa%d" % sub])

        def outproj_sub(i, sub, A):
            j = i * 4 + sub
            tsl = slice(sub * 128, (sub + 1) * 128)
            xsl = xrc[0] % 2
            xrc[0] += 1
            dma(xr[xsl][:], x_own[j * 128:(j + 1) * 128, :], [], ["xr%d" % xsl], "dxr%d" % xsl, add=A)
            for n, (bank, btok) in enumerate(((PF_, "PF"), (PG_, "PG"))):
                for k in range(8):
                    A("pe", lambda e, n=n, k=k, bank=bank: e.matmul(bank[:, :], lhsT=yT[:, k, tsl], rhs=woutb[:, k, n * 512:(n + 1) * 512],
                                                                   start=(k == 0), stop=(k == 7)), ["yTa%d" % sub, "yTb%d" % sub, "woutb"], [btok])
                A("dve", lambda e, n=n, bank=bank: e.tensor_tensor(out=xr[xsl][:, n * 512:(n + 1) * 512], in0=bank[:, :],
                                                                   in1=xr[xsl][:, n * 512:(n + 1) * 512], op=ALU.add),
                  [btok, "xr%d" % xsl], ["xr%d" % xsl])
            dma(x1s[j * 128:(j + 1) * 128, :], xr[xsl][:], ["xr%d" % xsl], ["x1s_%d" % j], "dx1_%d" % xsl, add=A)

        if stages >= 2.5:
            m0(0)
            for i_ in range(NST):
                m1(i_)

        btag = barrier()
        dma(gFb, gF.partition_broadcast(128), [btag], ["gFb"], "dconst2")
        def gu_chunk(f, dst3, src2, tok, use_act):
            sl = wl[0] % 3
            wl[0] += 1
            dma(stg[sl][:], src2[f * 128:(f + 1) * 128, :], [btag], ["stg%d" % sl], "dstg%d" % sl)
            o3 = dst3[:, :, f * 128:(f + 1) * 128]
            i3 = stg[sl][:].rearrange("p (k c) -> p k c", k=8)
            if use_act:
                S.add("act", lambda e: e.activation(out=o3, in_=i3, func=AF.Copy), ["stg%d" % sl, btag], ["%s%d" % (tok, f)])
            else:
                S.add("dve", lambda e: e.tensor_copy(out=o3, in_=i3), ["stg%d" % sl, btag], ["%s%d" % (tok, f)])

        def dn_chunk(f):
            sl = wl[0] % 3
            wl[0] += 1
            dma(stg[sl][:], w_down[f * 128:(f + 1) * 128, :], [btag], ["stg%d" % sl], "dstg%d" % sl)
            if f % 2 == 0:
                S.add("act", lambda e: e.activation(out=wdnb[:, f, :], in_=stg[sl][:], func=AF.Copy), ["stg%d" % sl, btag], ["wdn%d" % f])
            else:
                S.add("dve", lambda e: e.tensor_copy(out=wdnb[:, f, :], in_=stg[sl][:]), ["stg%d" % sl, btag], ["wdn%d" % f])

        fsl = {}

        def f0(i):
            for sub in range(4):
                j = i * 4 + sub
                fsl[j] = prep_load(x1s[j * 128:(j + 1) * 128, :], ["x1s_%d" % j])
                prep_transpose(fsl[j], xT[:, :, sub * 128:(sub + 1) * 128], "xT", gcol=8)

        oc = [0]

        def f1(i):
            gb = [(PA_, "PA"), (PB_, "PB")]
            ub = [(PC_, "PC"), (PD_, "PD")]
            for f in range(22):
                gbank, gtok = gb[f % 2]
                ubank, utok = ub[f % 2]
                if i == 0:
                    gu_chunk(f, wgb, w_gate, "wg", True)
                    gu_chunk(f, wub, w_up, "wu", False)
                    if f >= 11:
                        dn_chunk(2 * (f - 11))
                        dn_chunk(2 * (f - 11) + 1)
                for k in range(8):
                    S.add("pe", lambda e, f=f, k=k, gbank=gbank: e.matmul(gbank[:, :], lhsT=wgb[:, k, f * 128:(f + 1) * 128], rhs=xT[:, k, :],
                                                                         start=(k == 0), stop=(k == 7)), ["wg%d" % f, "xT"], [gtok])
                for k in range(8):
                    S.add("pe", lambda e, f=f, k=k, ubank=ubank: e.matmul(ubank[:, :], lhsT=wub[:, k, f * 128:(f + 1) * 128], rhs=xT[:, k, :],
                                                                         start=(k == 0), stop=(k == 7)), ["wu%d" % f, "xT"], [utok])
                S.add("act", lambda e, f=f, gbank=gbank: e.activation(out=sgs[f % 2], in_=gbank[:, :], func=AF.Silu), [gtok], ["sgs%d" % (f % 2)])
                S.add("dve", lambda e, f=f, ubank=ubank: e.tensor_tensor(out=actT[:, f, :], in0=ubank[:, :], in1=sgs[f % 2], op=ALU.mult),
                      [utok, "sgs%d" % (f % 2)], ["actT"])
            nxt = i + 1 < NST
            nsl = {}

            def nprep_load(sub):
                jn = (i + 1) * 4 + sub
                nsl[sub] = prep_load(x1s[jn * 128:(jn + 1) * 128, :], ["x1s_%d" % jn])

            if nxt:
                nprep_load(0)
                nprep_load(1)
            for sub in range(4):
                j = i * 4 + sub
                tsl = slice(sub * 128, (sub + 1) * 128)
                xsl = xrc[0] % 2
                xrc[0] += 1
                dma(xr[xsl][:], x1s[j * 128:(j + 1) * 128, :], ["x1s_%d" % j], ["xr%d" % xsl], "dxr%d" % xsl)
                dbanks = ((PE_, "PE"), (PF_, "PF")) if sub % 2 == 0 else ((PG_, "PG"), (PA_, "PA"))
                for n, (bank, btok) in enumerate(dbanks):
                    for f in range(22):
                        S.add("pe", lambda e, n=n, f=f, bank=bank, tsl=tsl: e.matmul(bank[:, :], lhsT=actT[:, f, tsl], rhs=wdnb[:, f, n * 512:(n + 1) * 512],
                                                                                    start=(f == 0), stop=(f == 21)), ["actT", "wdn%d" % f], [btok])
                    S.add("dve", lambda e, n=n, bank=bank, xsl=xsl: e.tensor_tensor(out=xr[xsl][:, n * 512:(n + 1) * 512], in0=bank[:, :],
                                                                                   in1=xr[xsl][:, n * 512:(n + 1) * 512], op=ALU.add),
                          [btok, "xr%d" % xsl], ["xr%d" % xsl])
                if nxt:
                    prep_transpose(nsl[sub], xT[:, :, sub * 128:(sub + 1) * 128], "xT", gcol=8)
                    if sub + 2 < 4:
                        nprep_load(sub + 2)
                S.add("act", lambda e, xsl=xsl: e.activation(out=junk[:], in_=xr[xsl][:], func=AF.Square, accum_out=mstB[:, 0:1]),
                      ["xr%d" % xsl], ["junk", "mstB"])
                S.add("act", lambda e: e.activation(out=mstB[:, 1:2], in_=mstB[:, 0:1], func=AF.Ln, scale=1.0 / D, bias=epsc[:, 0:1]),
                      ["mstB", "epsc"], ["mstB"])
                S.add("act", lambda e: e.activation(out=mstB[:, 2:3], in_=mstB[:, 1:2], func=AF.Exp, scale=-0.5), ["mstB"], ["mstB"])
                osl = oc[0] % 2
                oc[0] += 1
                S.add("dve", lambda e, osl=osl, xsl=xsl: e.scalar_tensor_tensor(out=stg[osl][:], in0=xr[xsl][:], scalar=mstB[:, 2:3], in1=gFb,
                                                                               op0=ALU.mult, op1=ALU.mult),
                      ["xr%d" % xsl, "mstB", "gFb"], ["stg%d" % osl])
                dma(out[j * 128:(j + 1) * 128, :], stg[osl][:], ["stg%d" % osl], [], "dout%d" % osl)

        if stages >= 5:
            f0(0)
            for i_ in range(NST):
                f1(i_)

        S.assign()
        dh = {n: es.enter_context(nc.semaphore("d_" + n)) for n in sorted(dnames)}
        with nc.Block() as block:
            @block.sync
            def _(eng):
                S.emit_engine("sp", eng, sems, dh, final_waits=[n for n in sorted(dnames)])

            @block.scalar
            def _(eng):
                S.emit_engine("act", eng, sems, dh)

            @block.vector
            def _(eng):
                S.emit_engine("dve", eng, sems, dh)

            @block.gpsimd
            def _(eng):
                S.emit_engine("pool", eng, sems, dh)

            @block.tensor
            def _(eng):
                S.emit_engine("pe", eng, sems, dh)
    return nc


def _consts():
    c = np.zeros((128, NCONST), np.float32)
    s = np.arange(128)[:, None]
    t = np.arange(128)[None, :]
    c[:, C_ID:C_ID + 128] = np.eye(128, dtype=np.float32)
    c[:, C_ONE:C_ONE + 128] = 1.0
    n16 = np.float32(-1.0 / 16.0)
    c[:, C_UN:C_UN + 128] = np.where(s <= t, n16, 0)
    c[:, C_UTN:C_UTN + 128] = np.where(s >= t, n16, 0)
    c[:, C_LSN:C_LSN + 128] = np.where(s > t, n16, 0)
    c[:, C_USN:C_USN + 128] = np.where(s < t, n16, 0)
    mf = np.where(s <= t, 1.0, 0.0).astype(np.float32)
    mb = np.where(s >= t, 1.0, 0.0).astype(np.float32)
    c[:, C_MF:C_MF + 512] = np.tile(mf, (1, 4))
    c[:, C_MB:C_MB + 512] = np.tile(mb, (1, 4))
    return c


def _fmajor(w):
    return np.ascontiguousarray(w.reshape(8, 128, 22, 128).transpose(2, 1, 0, 3).reshape(22 * 128, 8 * 128))


def _core_inputs(inp, b, half, T):
    f32 = lambda a: np.ascontiguousarray(np.asarray(a, dtype=np.float32))
    x = np.asarray(inp["x"], dtype=np.float32)
    w_in = np.asarray(inp["w_in"], dtype=np.float32)[0]
    q, k, v, g = w_in[:, 0:256], w_in[:, 256:512], w_in[:, 512:1024], w_in[:, 1024:1536]
    lrf, lrb = w_in[:, 1536:1552], w_in[:, 1552:1568]
    u, gvv = w_in[:, 1568:2080], w_in[:, 2080:2592]
    wdf, bdf = np.asarray(inp["w_decay_f"], np.float32)[0], np.asarray(inp["b_decay_f"], np.float32)[0]
    wdb, bdb = np.asarray(inp["w_decay_b"], np.float32)[0], np.asarray(inp["b_decay_b"], np.float32)[0]
    ws = np.asarray(inp["w_spatial"], np.float32)[0]
    bs = np.asarray(inp["b_spatial"], np.float32)[0]
    if half == 1:
        x_own, x_oth = x[b, T:2 * T], x[b, 0:T]
        LF, LB, WF, BF_, WB, BB = lrf, lrb, wdf, bdf, wdb, bdb
    else:
        x_own, x_oth = x[b, 0:T][::-1], x[b, T:2 * T][::-1]
        LF, LB, WF, BF_, WB, BB = lrb, lrf, wdb, bdb, wdf, bdf
        ws = ws[:, ::-1, ::-1]
        bs = bs[:, ::-1]
    w_in_dev = np.concatenate([q, k, g, u, k, LF, LB, v, gvv], axis=1)
    assert w_in_dev.shape == (D, NIN)
    wdblk = np.zeros((33, 512), np.float32)
    wdblk[0:16, 0:256] = WF
    wdblk[16:32, 256:512] = WB
    wdblk[32, 0:256] = BF_
    wdblk[32, 256:512] = BB
    pvec = np.zeros((128, 24), np.float32)
    pvec[:, 0:8] = np.asarray(inp["norm1_g"], np.float32)[0].reshape(8, 128).T
    pvec[:, 8:16] = np.asarray(inp["norm2_g"], np.float32)[0].reshape(8, 128).T
    pvec[:, 16:20] = np.asarray(inp["gla_norm_g"], np.float32)[0].reshape(4, 128).T
    pvec[:, 20:24] = np.asarray(inp["gmlp_ln_g"], np.float32)[0].reshape(4, 128).T
    rvec = np.zeros((1, 1024), np.float32)
    rvec[0, 0:512] = np.asarray(inp["gmlp_ln_b"], np.float32)[0]
    rvec[0, 512:1024] = bs.reshape(512)
    wsT = np.transpose(ws, (2, 0, 1)).reshape(128, 512)
    return {
        "x_own": f32(x_own), "x_oth": f32(x_oth), "w_in": f32(w_in_dev), "wdblk": wdblk,
        "consts": _consts(), "pvec": pvec, "rvec": rvec, "gF": f32(inp["final_norm_g"]),
        "wsT": f32(wsT), "w_out": f32(np.asarray(inp["w_out"], np.float32)[0]),
        "w_gate": f32(_fmajor(np.asarray(inp["w_gate"], np.float32)[0])), "w_up": f32(_fmajor(np.asarray(inp["w_up"], np.float32)[0])),
        "w_down": f32(np.asarray(inp["w_down"], np.float32)[0]),
    }


def kernel(**inputs):
    x = np.asarray(inputs["x"])
    B, SEQ, _ = x.shape
    T = SEQ // 2
    NT = T // 128
    ncores = 2 * B
    nc = build_nc(NT)
    in_maps = [_core_inputs(inputs, c // 2, c % 2, T) for c in range(ncores)]
    res = run_bass_kernel_spmd(nc, in_maps, core_ids=list(range(ncores)))
    outp = np.empty((B, SEQ, D), np.float32)
    for c in range(ncores):
        o = np.asarray(res.results[c]["out"], dtype=np.float32)
        b, half = c // 2, c % 2
        if half == 1:
            outp[b, T:2 * T] = o
        else:
            outp[b, 0:T] = o[::-1]
    return outp
```

```python
import numpy as np
from contextlib import ExitStack
import concourse.bass as bass
import concourse.mybir as mybir
from concourse.bass_utils import run_bass_kernel_spmd

F32 = mybir.dt.float32
BF16 = mybir.dt.bfloat16
AF = mybir.ActivationFunctionType
ALU = mybir.AluOpType

D = 1024
DFF = 2816
NFM = 1536
NTM = 1312
NIN = NFM + NTM
EPS = 1e-6
C_ID, C_ONE, C_UN, C_UTN, C_LSN, C_USN, C_MF, C_MB = 0, 128, 256, 384, 512, 640, 768, 1280
NCONST = 1792


class _Op:
    __slots__ = ("eng", "fn", "deps", "ddeps", "signal", "val", "dsem")


class Sched:
    ENG = ("pe", "act", "dve", "pool", "sp")

    def __init__(self):
        self.ops = {e: [] for e in self.ENG}
        self.tok = {}
        self.dsems = {}

    PSUM_TOK = ("PT", "PA", "PB", "PC", "PD", "PE", "PF", "PG")

    def add(self, eng, fn, reads=(), writes=(), dsem=None):
        pr = [t for t in reads if t in self.PSUM_TOK]
        if pr:
            reads = [t for t in reads if t not in self.PSUM_TOK]
            writes = list(writes) + pr
        op = _Op()
        op.eng, op.fn, op.signal, op.val, op.dsem = eng, fn, False, 0, dsem
        deps = set()
        for t in reads:
            st = self.tok.setdefault(t, [None, []])
            if st[0] is not None:
                deps.add(st[0])
        for t in writes:
            st = self.tok.setdefault(t, [None, []])
            if st[0] is not None:
                deps.add(st[0])
            deps.update(st[1])
        for t in reads:
            self.tok[t][1].append(op)
        for t in writes:
            self.tok[t][0] = op
            self.tok[t][1] = []
        deps.discard(op)
        op.deps, op.ddeps = [], {}
        for d in deps:
            if d.dsem is not None:
                op.ddeps[d.dsem] = self.dsems[d.dsem]
            elif d.eng == eng and eng == "pe":
                continue
            else:
                op.deps.append(d)
                d.signal = True
        if dsem is not None:
            self.dsems[dsem] = self.dsems.get(dsem, 0) + 16
        self.ops[eng].append(op)
        return op

    def assign(self):
        for e in self.ENG:
            c = 0
            for op in self.ops[e]:
                if op.dsem is None and op.signal:
                    c += 1
                    op.val = c

    def emit_engine(self, e, eng, sems, dh, final_waits=()):
        waited = {}
        for op in self.ops[e]:
            need = {}
            for d in op.deps:
                key = ("e", d.eng)
                if d.val > need.get(key, 0):
                    need[key] = d.val
            for k, v in op.ddeps.items():
                need[("d", k)] = v
            for key, v in need.items():
                if waited.get(key, 0) >= v:
                    continue
                waited[key] = v
                h = dh[key[1]] if key[0] == "d" else sems[key[1]]
                eng.wait_ge(h, v)
            ins = op.fn(eng)
            if op.dsem is not None:
                ins.then_inc(dh[op.dsem], 16)
            elif op.signal:
                ins.then_inc(sems[e], 1)
        for name in final_waits:
            eng.wait_ge(dh[name], self.dsems[name])


def _pipeline(n, stages, per_step=None, order=None):
    ns = len(stages)
    for step in range(n + ns - 1):
        for s in (order if order is not None else reversed(range(ns))):
            i = step - s
            if 0 <= i < n:
                stages[s](i)
        if per_step is not None:
            per_step(step, n + ns - 1)


def build_nc(NT=32, stages=9):
    T = NT * 128
    NST = NT // 4
    nc = bass.Bass("TRN2", target_bir_lowering=False)
    dt_in = lambda name, shape: nc.dram_tensor(name, shape, F32, kind="ExternalInput").ap()
    x_own = dt_in("x_own", [T, D])
    x_oth = dt_in("x_oth", [T, D])
    w_in = dt_in("w_in", [D, NIN])
    wdblk = dt_in("wdblk", [33, 512])
    consts = dt_in("consts", [128, NCONST])
    pvec = dt_in("pvec", [128, 24])
    rvec = dt_in("rvec", [1, 1024])
    gF = dt_in("gF", [D])
    wsT = dt_in("wsT", [128, 512])
    w_out = dt_in("w_out", [D, D])
    w_gate = dt_in("w_gate", [DFF, D])
    w_up = dt_in("w_up", [DFF, D])
    w_down = dt_in("w_down", [DFF, D])
    out = nc.dram_tensor("out", [T, D], F32, kind="ExternalOutput").ap()
    x1s = nc.dram_tensor("x1s", [T, D], F32).ap()

    S = Sched()
    dnames = set()

    def dma(outap, inap, reads, writes, dsem, add=None, eng="sp"):
        dnames.add(dsem)
        return (add or S.add)(eng, lambda e: e.dma_start(out=outap, in_=inap), reads, writes, dsem)

    with ExitStack() as es:
        sb = lambda name, shape, dt: es.enter_context(nc.sbuf_tensor(name, shape, dt))
        PT = es.enter_context(nc.psum_tensor("PT", [128, 1024], BF16))
        PB = [es.enter_context(nc.psum_tensor("PB%d" % i, [128, 512], F32)) for i in range(7)]
        PA_, PB_, PC_, PD_, PE_, PF_, PG_ = PB
        cstI = sb("cstI", [128, 256], F32)
        idb = sb("idb", [128, 128], BF16)
        oneb = sb("oneb", [128, 128], BF16)
        pv = sb("pv", [128, 24], F32)
        epsc = sb("epsc", [128, 1], F32)
        bsc = sb("bsc", [128, 16], F32)
        stg = [sb("stg%d" % i, [128, 1024], F32) for i in range(3)]
        xs = [sb("xs%d" % i, [128, D], F32) for i in range(2)]
        xr = [sb("xr%d" % i, [128, D], F32) for i in range(2)]
        junk = sb("junk", [128, D], BF16)
        hb = [sb("hb%d" % i, [128, D], BF16) for i in range(2)]
        ss = [sb("ss%d" % i, [128, 2], F32) for i in range(2)]
        rstd = [sb("rstd%d" % i, [128, 1], F32) for i in range(2)]
        xT = sb("xT", [128, 8, 512], BF16)
        ARENA_F32 = 41984
        arena = sb("arena", [128, ARENA_F32], F32)
        cur = [0]

        def carve(ncols_elem, dt):
            nf = ncols_elem if dt == F32 else (ncols_elem + 1) // 2
            a = arena[:, cur[0]:cur[0] + nf]
            cur[0] += nf
            assert cur[0] <= ARENA_F32, cur[0]
            if dt != F32:
                a = a.bitcast(dt)
            return a

        cur[0] = 0
        winb = carve(8 * NIN, BF16).rearrange("p (k c) -> p k c", k=8)
        woutb = carve(8 * D, BF16).rearrange("p (k c) -> p k c", k=8)
        Sbst = carve(NT * 256, BF16).rearrange("p (j q v) -> p j q v", j=NT, q=2)
        cst = carve(NCONST, F32)
        wdb = carve(512, F32)[0:33, :]
        wsb = carve(512, BF16)
        Cg = carve(512, F32)
        lrT = [carve(128, F32)[0:33, :] for _ in range(4)]
        Sst = carve(256, F32).rearrange("p (q v) -> p q v", q=2)
        Sf = carve(256, F32).rearrange("p (q v) -> p q v", q=2)
        Sfb = carve(256, BF16).rearrange("p (q v) -> p q v", q=2)
        decs = [carve(2, F32) for _ in range(2)]
        mst = carve(8, F32)
        union0 = cur[0]
        rv = carve(1024, F32)[0:1, :]
        wsf = carve(512, F32)
        rsrow = carve(512, F32)[0:1, :]
        cur[0] = union0
        NSL = 4
        xTp = [carve(1024, BF16).rearrange("p (k t) -> p k t", k=8) for _ in range(NSL)]
        p_vb = [carve(512, BF16) for _ in range(NSL)]
        p_ktm = [carve(256, F32) for _ in range(NSL)]
        p_lr = [carve(32, F32) for _ in range(NSL)]
        p_lap = [carve(256, F32) for _ in range(NSL)]
        p_Eend = [carve(256, F32) for _ in range(NSL)]
        p_kte = [carve(256, BF16) for _ in range(NSL)]
        cur[0] = union0
        qT = carve(1024, F32).rearrange("p (q t) -> p q t", q=2)
        kT = carve(1024, F32).rearrange("p (q t) -> p q t", q=2)
        sgT = carve(2048, F32).rearrange("p (h t) -> p h t", h=4)
        uT = carve(2048, F32).rearrange("p (h t) -> p h t", h=4)
        vb = carve(4 * 512, BF16).rearrange("p (s c) -> p s c", s=4)
        gv = carve(4 * 512, F32).rearrange("p (s c) -> p s c", s=4)
        ktm = carve(4 * 256, F32).rearrange("p (s c) -> p s c", s=4)
        lrs = carve(4 * 32, F32).rearrange("p (s c) -> p s c", s=4)
        lnst = carve(4 * 2, F32).rearrange("p (s c) -> p s c", s=4)
        yT = carve(8 * 512, BF16).rearrange("p (k t) -> p k t", k=8)
        lap4 = [carve(512, F32) for _ in range(4)]
        lap0 = lap4[0]
        Ee0 = carve(512, F32)
        Ei0 = carve(512, F32)
        qdec0 = carve(512, BF16)
        kdec0 = carve(512, BF16)
        scF0 = carve(512, BF16)
        scB0 = carve(512, BF16)
        oTs0 = carve(512, F32)
        lap1, Ee1 = stg[0][:, 0:512], stg[0][:, 512:1024]
        Ei1 = stg[1][:, 0:512]
        qdec1 = stg[1][:, 512:768].bitcast(BF16)
        kdec1 = stg[1][:, 768:1024].bitcast(BF16)
        scF1 = stg[2][:, 0:256].bitcast(BF16)
        scB1 = stg[2][:, 256:512].bitcast(BF16)
        oTs1 = stg[2][:, 512:1024]
        GB = []
        for (lap_, Ee_, Ei_, qd_, kd_, sF_, sB_, oT_) in ((lap0, Ee0, Ei0, qdec0, kdec0, scF0, scB0, oTs0),
                                                         (lap1, Ee1, Ei1, qdec1, kdec1, scF1, scB1, oTs1)):
            GB.append(dict(lap=lap_, Ee=Ee_, Ei=Ei_, qdec=qd_.rearrange("p (d c) -> p d c", d=2),
                           kdec=kd_.rearrange("p (d c) -> p d c", d=2), scF=sF_, scB=sB_, oTs=oT_,
                           Eend=lap_[:, 0:256], rsb=Ei_, osq=sF_, kte=kd_[:, 0:256]))
        vn = carve(512, BF16)
        tmpg = carve(512, F32)
        endA = cur[0]
        cur[0] = 0
        wgb = carve(8 * DFF, BF16).rearrange("p (k c) -> p k c", k=8)
        wub = carve(8 * DFF, BF16).rearrange("p (k c) -> p k c", k=8)
        wdnb = carve(22 * D, BF16).rearrange("p (f c) -> p f c", f=22)
        actT = carve(22 * 512, BF16).rearrange("p (f t) -> p f t", f=22)
        sgs = [carve(512, F32) for _ in range(2)]
        gFb = carve(D, F32)
        mstB = carve(8, F32)
        endB = cur[0]
        assert max(endA, endB) <= ARENA_F32, (endA, endB)
        print("arena use (f32 cols): phaseA %d phaseB %d of %d" % (endA, endB, ARENA_F32))

        sems = {e: es.enter_context(nc.semaphore("s_" + e)) for e in Sched.ENG}

        bcount = [0]

        def barrier(extra_reads=()):
            n = bcount[0]
            bcount[0] += 1
            tag = "bar%d" % n
            S.add("act", lambda e: e.activation(out=bsc[:, 0:1], in_=epsc[:, 0:1], func=AF.Copy), ["epsc"], [tag + "a"])
            S.add("pool", lambda e: e.memset(bsc[:, 1:2], 0.0), [], [tag + "p"])
            S.add("pe", lambda e: e.matmul(PA_[0:1, 0:1], lhsT=cstI[:, 128:129], rhs=cstI[:, 128:129], start=True, stop=True),
                  ["cstI"], ["PA"])
            S.add("dve", lambda e: e.tensor_copy(out=bsc[0:1, 2:3], in_=PA_[0:1, 0:1]),
                  ["PA", tag + "a", tag + "p"] + list(extra_reads), [tag])
            S.add("act", lambda e: e.activation(out=bsc[:, 3:4], in_=epsc[:, 0:1], func=AF.Copy), ["epsc", tag], [tag + "ra"])
            S.add("pool", lambda e: e.memset(bsc[:, 4:5], 0.0), [tag], [tag + "rp"])
            S.add("pe", lambda e: e.matmul(PA_[0:1, 0:1], lhsT=cstI[:, 128:129], rhs=cstI[:, 128:129], start=True, stop=True),
                  ["cstI", tag], ["PA"])
            return tag

        dma(cstI[:], consts[:, 0:256], [], ["cstI"], "dconst")
        dma(cst[:], consts, [], ["cst"], "dconst")
        dma(pv[:], pvec, [], ["pv"], "dconst")
        dma(rv, rvec, [], ["rv"], "dconst")
        dma(wdb, wdblk, [], ["wdb"], "dconst")
        dma(wsf, wsT, [], ["wsf"], "dconst")
        S.add("dve", lambda e: e.memset(epsc[:], EPS), [], ["epsc"])
        S.add("dve", lambda e: e.tensor_copy(out=idb[:], in_=cstI[:, 0:128]), ["cstI"], ["idb"])
        S.add("dve", lambda e: e.tensor_copy(out=oneb[:], in_=cstI[:, 128:256]), ["cstI"], ["oneb"])
        S.add("dve", lambda e: e.tensor_copy(out=wsb, in_=wsf), ["wsf"], ["wsb"])
        for i in range(4):
            S.add("dve", lambda e, i=i: e.memset(lrT[i][32:33, :], 1.0), [], ["lrT1_%d" % i])
        S.add("dve", lambda e: e.memset(Sst, 0.0), [], ["Sst"])
        S.add("pe", lambda e: e.matmul(PA_[0:1, :], lhsT=cstI[:, 128:129], rhs=wsf, start=True, stop=True),
              ["cstI", "wsf"], ["PA"])
        S.add("dve", lambda e: e.tensor_copy(out=rsrow, in_=PA_[0:1, :]), ["PA"], ["rsrow"])
        for g in range(4):
            S.add("pe", lambda e, g=g: e.matmul(PB_[:, g * 128:(g + 1) * 128], lhsT=rv[0:1, g * 128:(g + 1) * 128],
                                                rhs=rsrow[0:1, g * 128:(g + 1) * 128], start=True, stop=False),
                  ["rv", "rsrow"], ["PB"])
            S.add("pe", lambda e, g=g: e.matmul(PB_[:, g * 128:(g + 1) * 128], lhsT=cstI[0:1, 128:256],
                                                rhs=rv[0:1, 512 + g * 128:512 + (g + 1) * 128], start=False, stop=True),
                  ["rv", "cstI"], ["PB"])
        S.add("dve", lambda e: e.tensor_copy(out=Cg, in_=PB_[:, :]), ["PB"], ["Cg"])

        wl = [0]

        def weight_chunks(dst3, src2, ktiles, col_ranges, scale_col0=None, tokname="w", extra_reads=(), weng="sp"):
            th = []
            for k in range(ktiles):
                for (lo, hi) in col_ranges:
                    c0 = lo
                    while c0 < hi:
                        cw = min(1024, hi - c0)

                        def thunk(k=k, c0=c0, cw=cw):
                            sl = wl[0] % 3
                            wl[0] += 1
                            use_act = (wl[0] % 2) == 0
                            dma(stg[sl][:, 0:cw], src2[k * 128:(k + 1) * 128, c0:c0 + cw], list(extra_reads), ["stg%d" % sl], "dstg%d%s" % (sl, weng),
                                eng=weng)
                            rd = ["stg%d" % sl] + list(extra_reads)
                            if scale_col0 is None:
                                if use_act:
                                    S.add("act", lambda e: e.activation(out=dst3[:, k, c0:c0 + cw], in_=stg[sl][:, 0:cw], func=AF.Copy), rd, [tokname])
                                else:
                                    S.add("dve", lambda e: e.tensor_copy(out=dst3[:, k, c0:c0 + cw], in_=stg[sl][:, 0:cw]), rd, [tokname])
                            else:
                                sc = pv[:, scale_col0 + k:scale_col0 + k + 1]
                                if use_act:
                                    S.add("act", lambda e: e.activation(out=dst3[:, k, c0:c0 + cw], in_=stg[sl][:, 0:cw], func=AF.Copy, scale=sc),
                                          rd + ["pv"], [tokname])
                                else:
                                    S.add("dve", lambda e: e.tensor_scalar(out=dst3[:, k, c0:c0 + cw], in0=stg[sl][:, 0:cw], scalar1=sc,
                                                                           scalar2=None, op0=ALU.mult), rd + ["pv"], [tokname])
                        th.append(thunk)
                        c0 += cw
            return th

        barrier()
        for t_ in weight_chunks(winb, w_in, 8, [(NFM, NFM + 800)], tokname="winb_tm"):
            t_()
        late_w = (weight_chunks(winb, w_in, 8, [(0, NFM), (NFM + 800, NIN)], tokname="winb_rest", weng="pool")
                  + weight_chunks(woutb, w_out, 8, [(0, D)], tokname="woutb", weng="pool"))

        def late_hook(step, nsteps):
            want = (step + 1) * len(late_w) // max(1, nsteps - 6)
            while late_hook.i < min(want, len(late_w)):
                late_w[late_hook.i]()
                late_hook.i += 1
        late_hook.i = 0

        pc = [0]

        def prep_load(src_rows, src_reads=(), add=None):
            sl = pc[0] % 2
            pc[0] += 1
            dma(xs[sl][:], src_rows, list(src_reads), ["xs%d" % sl], "dxs%d" % sl, add=add)
            add = add or S.add
            add("act", lambda e: e.activation(out=junk[:], in_=xs[sl][:], func=AF.Square, accum_out=ss[sl][:, 0:1]),
                  ["xs%d" % sl], ["junk", "ss%d" % sl])
            add("act", lambda e: e.activation(out=ss[sl][:, 1:2], in_=ss[sl][:, 0:1], func=AF.Ln, scale=1.0 / D, bias=epsc[:, 0:1]),
                  ["ss%d" % sl, "epsc"], ["ss%d" % sl])
            add("act", lambda e: e.activation(out=rstd[sl][:], in_=ss[sl][:, 1:2], func=AF.Exp, scale=-0.5),
                  ["ss%d" % sl], ["rstd%d" % sl])
            add("act", lambda e: e.activation(out=hb[sl][:], in_=xs[sl][:], func=AF.Copy, scale=rstd[sl][:, 0:1]),
                  ["xs%d" % sl, "rstd%d" % sl], ["hb%d" % sl])
            return sl

        def prep_transpose(sl, dst3, dtok, gcol=0, add=None):
            add = add or S.add
            for k in range(8):
                add("pe", lambda e, k=k: e.transpose(out=PT[:, k * 128:(k + 1) * 128], in_=hb[sl][:, k * 128:(k + 1) * 128], identity=idb[:]),
                      ["hb%d" % sl, "idb"], ["PT"])
            add("dve", lambda e: e.tensor_tensor(out=dst3, in0=PT[:].rearrange("p (k t) -> p k t", k=8),
                                                   in1=pv[:, gcol:gcol + 8].unsqueeze(2).to_broadcast([128, 8, 128]), op=ALU.mult),
                  ["PT", "pv"], [dtok])

        def la_from_lr(lr_ap, lr_tok, slot, zcols, lap_ap, lap_tok, zbank, zbank_tok, add=None):
            add = add or S.add
            add("pe", lambda e: e.transpose(out=zbank[0:32, 0:128], in_=lr_ap, identity=cstI[:, 0:128]),
                [lr_tok, "cstI"], [zbank_tok])
            add("dve", lambda e: e.tensor_copy(out=lrT[slot][0:32, :], in_=zbank[0:32, 0:128]), [zbank_tok], ["lrT%d" % slot])
            c0, c1 = zcols
            add("pe", lambda e: e.matmul(zbank[:, c0:c1], lhsT=lrT[slot][:, :], rhs=wdb[:, c0:c1], start=True, stop=True),
                ["lrT%d" % slot, "lrT1_%d" % slot, "wdb"], [zbank_tok])
            add("act", lambda e: e.activation(out=lap_ap, in_=zbank[:, c0:c1], func=AF.Exp, scale=-1.0), [zbank_tok], [lap_tok])
            add("act", lambda e: e.activation(out=lap_ap, in_=lap_ap, func=AF.Ln, bias=1.0), [lap_tok], [lap_tok])

        def state_update(Sm, Sm_tok, dps, dps_tok, dec_ap, dec_tok, add=None):
            add = add or S.add
            for p in range(2):
                for r in range(2):
                    add("dve", lambda e, p=p, r=r: e.scalar_tensor_tensor(
                        out=Sm[r * 64:(r + 1) * 64, p, :], in0=Sm[r * 64:(r + 1) * 64, p, :],
                        scalar=dec_ap[r * 64:(r + 1) * 64, p:p + 1],
                        in1=dps[r * 64:(r + 1) * 64, p * 256 + r * 128:p * 256 + (r + 1) * 128],
                        op0=ALU.mult, op1=ALU.add), [Sm_tok, dps_tok, dec_tok], [Sm_tok])

        def scan_items(src, order, dirn, store, Sm, Sm_tok):
            zc = (dirn * 256, dirn * 256 + 256)
            tmat = C_LSN if dirn == 0 else C_USN
            return [(src, j, zc, tmat, store, Sm, Sm_tok) for j in order]

        items = []
        if stages >= 1:
            it1 = scan_items(x_oth, list(range(NT)), 0, False, Sf, "Sf")
            it2 = scan_items(x_own, list(range(NT - 1, -1, -1)), 1, True, Sst, "Sst") if stages >= 2 else []
            for a_, b_ in zip(it1, it2 if it2 else [None] * len(it1)):
                items.append(a_)
                if b_ is not None:
                    items.append(b_)
        S.add("dve", lambda e: e.memset(Sf, 0.0), [], ["Sf"])
        sls = {}

        def st0(i):
            src, j = items[i][0], items[i][1]
            sls[i] = prep_load(src[j * 128:(j + 1) * 128, :])

        def st1a(i):
            s = i % NSL
            prep_transpose(sls[i], xTp[s], "xTp%d" % s)

        def st1b(i):
            s = i % NSL
            for k in range(8):
                S.add("pe", lambda e, k=k: e.matmul(PA_[:, 0:288], lhsT=xTp[s][:, k, :], rhs=winb[:, k, NFM:NFM + 288],
                                                    start=(k == 0), stop=(k == 7)), ["xTp%d" % s, "winb_tm"], ["PA"])
            for k in range(8):
                S.add("pe", lambda e, k=k: e.matmul(PB_[:, :], lhsT=xTp[s][:, k, :], rhs=winb[:, k, NFM + 288:NFM + 800],
                                                    start=(k == 0), stop=(k == 7)), ["xTp%d" % s, "winb_tm"], ["PB"])
            S.add("dve", lambda e: e.tensor_copy(out=p_ktm[s], in_=PA_[:, 0:256]), ["PA"], ["p_ktm%d" % s])
            S.add("dve", lambda e: e.tensor_copy(out=p_lr[s], in_=PA_[:, 256:288]), ["PA"], ["p_lr%d" % s])
            S.add("dve", lambda e: e.tensor_copy(out=p_vb[s], in_=PB_[:, :]), ["PB"], ["p_vb%d" % s])

        def st1c1(i):
            s = i % NSL
            q = i % 2
            S.add("pe", lambda e: e.transpose(out=PC_[0:32, 0:128], in_=p_lr[s], identity=cstI[:, 0:128]), ["p_lr%d" % s, "cstI"], ["PC"])
            S.add("dve", lambda e: e.tensor_copy(out=lrT[q][0:32, :], in_=PC_[0:32, 0:128]), ["PC"], ["lrT%d" % q])

        def st1c2(i):
            s = i % NSL
            q = i % 2
            c0, c1 = items[i][2]
            S.add("pe", lambda e: e.matmul(PC_[:, c0:c1], lhsT=lrT[q][:, :], rhs=wdb[:, c0:c1], start=True, stop=True),
                  ["lrT%d" % q, "lrT1_%d" % q, "wdb"], ["PC"])
            S.add("act", lambda e: e.activation(out=p_lap[s], in_=PC_[:, c0:c1], func=AF.Exp, scale=-1.0), ["PC"], ["p_lap%d" % s])
            S.add("act", lambda e: e.activation(out=p_lap[s], in_=p_lap[s], func=AF.Ln, bias=1.0), ["p_lap%d" % s], ["p_lap%d" % s])

        def st2a(i):
            s = i % NSL
            d = i % 2
            src, j, zc, tmat, store, Sm, Sm_tok = items[i]
            S.add("pe", lambda e: e.matmul(PD_[:, 0:256], lhsT=cst[:, tmat:tmat + 128], rhs=p_lap[s], start=True, stop=True),
                  ["cst", "p_lap%d" % s], ["PD"])
            for p in range(2):
                S.add("pe", lambda e, p=p: e.matmul(PD_[:, 256 + p:257 + p], lhsT=p_lap[s][:, p * 128:(p + 1) * 128],
                                                    rhs=cst[:, C_UN + 127:C_UN + 128], start=True, stop=True),
                      ["cst", "p_lap%d" % s], ["PD"])
            S.add("act", lambda e: e.activation(out=p_Eend[s], in_=PD_[:, 0:256], func=AF.Exp), ["PD"], ["p_Eend%d" % s])
            S.add("act", lambda e: e.activation(out=decs[d], in_=PD_[:, 256:258], func=AF.Exp), ["PD"], ["decs%d" % d])
            S.add("dve", lambda e: e.tensor_tensor(out=p_kte[s], in0=p_ktm[s], in1=p_Eend[s], op=ALU.mult),
                  ["p_ktm%d" % s, "p_Eend%d" % s], ["p_kte%d" % s])

        def st2b(i):
            s = i % NSL
            d = i % 2
            src, j, zc, tmat, store, Sm, Sm_tok = items[i]
            for p in range(2):
                S.add("pe", lambda e, p=p: e.matmul(PE_[:, p * 256:(p + 1) * 256], lhsT=p_kte[s][:, p * 128:(p + 1) * 128],
                                                    rhs=p_vb[s][:, p * 256:(p + 1) * 256], start=True, stop=True),
                      ["p_kte%d" % s, "p_vb%d" % s], ["PE"])
            if store:
                S.add("act", lambda e: e.activation(out=Sbst[:, j, :, :], in_=Sm, func=AF.Copy), [Sm_tok], ["Sbst%d" % j])
            state_update(Sm, Sm_tok, PE_, "PE", decs[d], "decs%d" % d)

        sched_ = [(st1c1, 3), (st2b, 5), (st2a, 4), (st1c2, 3), (st1b, 2), (st1a, 1), (st0, 0)]
        nst_ = len(items) + 5
        for step_ in range(nst_):
            for fn_, lag_ in sched_:
                i_ = step_ - lag_
                if 0 <= i_ < len(items):
                    fn_(i_)
            late_hook(step_, nst_)
        while late_hook.i < len(late_w):
            late_w[late_hook.i]()
            late_hook.i += 1
        S.add("act", lambda e: e.activation(out=Sfb, in_=Sf, func=AF.Copy), ["Sf"], ["Sfb"])
        barrier()

        xrc = [0]
        sub_sl = {}

        def m0(i):
            for sub in range(4):
                j = i * 4 + sub
                sub_sl[j] = prep_load(x_own[j * 128:(j + 1) * 128, :])
                prep_transpose(sub_sl[j], xT[:, :, sub * 128:(sub + 1) * 128], "xT")

        def m1(i):
            fm_banks = [(PF_, "PF"), (PG_, "PG")]
            for c in range(12):
                bank, btok = fm_banks[c % 2]
                for k in range(8):
                    S.add("pe", lambda e, c=c, k=k, bank=bank: e.matmul(bank[:, :], lhsT=winb[:, k, c * 128:(c + 1) * 128], rhs=xT[:, k, :],
                                                                       start=(k == 0), stop=(k == 7)), ["winb_tm", "winb_rest", "xT"], [btok])
                if c < 2:
                    S.add("act", lambda e, c=c, bank=bank: e.activation(out=qT[:, c, :], in_=bank[:, :], func=AF.Copy), [btok], ["qT"])
                elif c < 4:
                    S.add("act", lambda e, c=c, bank=bank: e.activation(out=kT[:, c - 2, :], in_=bank[:, :], func=AF.Copy), [btok], ["kT"])
                elif c < 8:
                    S.add("act", lambda e, c=c, bank=bank: e.activation(out=sgT[:, c - 4, :], in_=bank[:, :], func=AF.Silu), [btok], ["sgT"])
                else:
                    S.add("act", lambda e, c=c, bank=bank: e.activation(out=uT[:, c - 8, :], in_=bank[:, :], func=AF.Gelu), [btok], ["uT"])
            for sub in range(4):
                tsl = slice(sub * 128, (sub + 1) * 128)
                for (bank, btok, c0, cw) in ((PA_, "PA", 0, 288), (PB_, "PB", 288, 512), (PC_, "PC", 800, 512)):
                    for k in range(8):
                        S.add("pe", lambda e, k=k, bank=bank, c0=c0, cw=cw, tsl=tsl: e.matmul(
                            bank[:, 0:cw], lhsT=xT[:, k, tsl], rhs=winb[:, k, NFM + c0:NFM + c0 + cw],
                            start=(k == 0), stop=(k == 7)), ["winb_tm", "winb_rest", "xT"], [btok])
                S.add("dve", lambda e, sub=sub: e.tensor_copy(out=ktm[:, sub, :], in_=PA_[:, 0:256]), ["PA"], ["ktm%d" % sub])
                S.add("dve", lambda e, sub=sub: e.tensor_copy(out=lrs[:, sub, :], in_=PA_[:, 256:288]), ["PA"], ["lrs%d" % sub])
                S.add("dve", lambda e, sub=sub: e.tensor_copy(out=vb[:, sub, :], in_=PB_[:, :]), ["PB"], ["vb%d" % sub])
                S.add("act", lambda e, sub=sub: e.activation(out=gv[:, sub, :], in_=PC_[:, :], func=AF.Gelu,
                                                             accum_out=lnst[:, sub, 0:1]), ["PC"], ["gv%d" % sub, "lnst%d" % sub])
            las = []
            for sub in range(4):
                l_ = []
                la_from_lr(lrs[:, sub, :], "lrs%d" % sub, sub, (0, 512), lap4[sub], "lap4_%d" % sub,
                           (PA_, PB_, PC_, PD_)[sub], ("PA", "PB", "PC", "PD")[sub],
                           add=lambda *a, l_=l_, **k: l_.append(lambda: S.add(*a, **k)))
                las.append(l_)
            for k_ in range(len(las[0])):
                for sub in range(4):
                    las[sub][k_]()
            glas, gms, ops_ = [], [], []
            for sub in range(4):
                g_, m_, o_ = [], [], []
                gla_sub(i, sub, lambda *a, g_=g_, **k: g_.append(lambda: S.add(*a, **k)))
                gmlp_sub(i, sub, lambda *a, m_=m_, **k: m_.append(lambda: S.add(*a, **k)))
                if sub > 0:
                    outproj_sub(i, sub - 1, lambda *a, o_=o_, **k: o_.append(lambda: S.add(*a, **k)))
                if i + 1 < NST:
                    jn = (i + 1) * 4 + sub
                    A_ = lambda *a, o_=o_, **k: o_.append(lambda: S.add(*a, **k))
                    sln = prep_load(x_own[jn * 128:(jn + 1) * 128, :], add=A_)
                    prep_transpose(sln, xT[:, :, sub * 128:(sub + 1) * 128], "xT", add=A_)
                glas.append(g_)
                gms.append(m_)
                ops_.append(o_)
            L = len(glas[0])
            H = (L + 1) // 2
            mi = [0] * 4
            oi = [0] * 4
            for tick in range(H * 3 + L):
                for sub in range(4):
                    k = tick - sub * H
                    if 0 <= k < L:
                        glas[sub][k]()
                        if k < H:
                            want = (k + 1) * len(gms[sub]) // H
                            while mi[sub] < want:
                                gms[sub][mi[sub]]()
                                mi[sub] += 1
                        else:
                            want = (k + 1 - H) * len(ops_[sub]) // (L - H)
                            while oi[sub] < want:
                                ops_[sub][oi[sub]]()
                                oi[sub] += 1
            for sub in range(4):
                assert mi[sub] == len(gms[sub]) and oi[sub] == len(ops_[sub])
            outproj_sub(i, 3, S.add)

        def gmlp_sub(i, sub, A):
            tsl = slice(sub * 128, (sub + 1) * 128)
            A("act", lambda e: e.activation(out=vn, in_=gv[:, sub, :], func=AF.Square,
                                            accum_out=lnst[:, sub, 1:2]), ["gv%d" % sub], ["vn", "lnst%d" % sub])
            A("dve", lambda e: e.tensor_scalar(out=mst[:, 0:2], in0=lnst[:, sub, :], scalar1=1.0 / 512, scalar2=None, op0=ALU.mult),
              ["lnst%d" % sub], ["mst"])
            A("dve", lambda e: e.tensor_tensor(out=mst[:, 2:3], in0=mst[:, 0:1], in1=mst[:, 0:1], op=ALU.mult), ["mst"], ["mst"])
            A("dve", lambda e: e.tensor_tensor(out=mst[:, 3:4], in0=mst[:, 1:2], in1=mst[:, 2:3], op=ALU.subtract), ["mst"], ["mst"])
            A("act", lambda e: e.activation(out=mst[:, 4:5], in_=mst[:, 3:4], func=AF.Ln, bias=epsc[:, 0:1]), ["mst", "epsc"], ["mst"])
            A("act", lambda e: e.activation(out=mst[:, 5:6], in_=mst[:, 4:5], func=AF.Exp, scale=-0.5), ["mst"], ["mst"])
            A("dve", lambda e: e.tensor_scalar(out=vn, in0=gv[:, sub, :], scalar1=mst[:, 0:1], scalar2=mst[:, 5:6],
                                               op0=ALU.subtract, op1=ALU.mult), ["gv%d" % sub, "mst"], ["vn"])
            for g in range(4):
                A("pe", lambda e, g=g: e.matmul(PF_[:, g * 128:(g + 1) * 128], lhsT=vn[:, g * 128:(g + 1) * 128],
                                                rhs=wsb[:, g * 128:(g + 1) * 128], start=True, stop=True), ["vn", "wsb"], ["PF"])
            for g in range(4):
                A("dve", lambda e, g=g: e.scalar_tensor_tensor(
                    out=tmpg[:, g * 128:(g + 1) * 128], in0=PF_[:, g * 128:(g + 1) * 128], scalar=pv[:, 20 + g:21 + g],
                    in1=Cg[:, g * 128:(g + 1) * 128], op0=ALU.mult, op1=ALU.add), ["PF", "pv", "Cg"], ["tmpg"])
            A("dve", lambda e: e.tensor_tensor(out=yT[:, 4:8, tsl], in0=tmpg.rearrange("p (g t) -> p g t", g=4),
                                               in1=uT[:, :, tsl], op=ALU.mult), ["tmpg", "uT"], ["yTb%d" % sub])

        def gla_sub(i, sub, A):
            j = i * 4 + sub
            q = sub % 2
            B_ = GB[q]
            lap, Ee, Ei, qdec, kdec, scF, scB, oTs = (B_[n] for n in ("lap", "Ee", "Ei", "qdec", "kdec", "scF", "scB", "oTs"))
            Eend, rsb, osq, kte = B_["Eend"], B_["rsb"], B_["osq"], B_["kte"]
            T = lambda n: "%s_%d" % (n, q)
            tsl = slice(sub * 128, (sub + 1) * 128)
            lap = lap4[sub]
            Eend = lap[:, 0:256]
            for dirn in range(2):
                um = C_UN if dirn == 0 else C_UTN
                for p in range(2):
                    A("pe", lambda e, dirn=dirn, p=p, um=um: e.matmul(
                        PB_[:, (dirn * 2 + p) * 128:(dirn * 2 + p + 1) * 128], lhsT=lap[:, dirn * 256 + p * 128:dirn * 256 + (p + 1) * 128],
                        rhs=cst[:, um:um + 128], start=True, stop=True), ["lap4_%d" % sub, "cst"], ["PB"])
            A("act", lambda e: e.activation(out=Ee, in_=PB_[:, :], func=AF.Exp), ["PB"], [T("Ee")])
            A("act", lambda e: e.activation(out=Ei, in_=PB_[:, :], func=AF.Exp, scale=-1.0), ["PB"], [T("Ei")])
            for dirn in range(2):
                A("dve", lambda e, dirn=dirn: e.scalar_tensor_tensor(
                    out=qdec[:, dirn, :].rearrange("p (q t) -> p q t", q=2), in0=qT[:, :, tsl], scalar=0.125,
                    in1=Ee[:, dirn * 256:(dirn + 1) * 256].rearrange("p (q t) -> p q t", q=2), op0=ALU.mult, op1=ALU.mult),
                    ["qT", T("Ee")], [T("qdec")])
                A("dve", lambda e, dirn=dirn: e.tensor_tensor(
                    out=kdec[:, dirn, :].rearrange("p (q t) -> p q t", q=2), in0=kT[:, :, tsl],
                    in1=Ei[:, dirn * 256:(dirn + 1) * 256].rearrange("p (q t) -> p q t", q=2), op=ALU.mult), ["kT", T("Ei")], [T("kdec")])
            for dirn in range(2):
                for p in range(2):
                    for r, (bank, btok) in enumerate(((PC_, "PC"), (PD_, "PD"))):
                        A("pe", lambda e, dirn=dirn, p=p, r=r, bank=bank: e.matmul(
                            bank[:, (dirn * 2 + p) * 128:(dirn * 2 + p + 1) * 128], lhsT=kdec[r * 64:(r + 1) * 64, dirn, p * 128:(p + 1) * 128],
                            rhs=qdec[r * 64:(r + 1) * 64, dirn, p * 128:(p + 1) * 128], start=True, stop=True),
                            [T("kdec"), T("qdec")], [btok])
            A("dve", lambda e: e.tensor_tensor(out=scF, in0=PC_[:, :], in1=cst[:, C_MF + 256:C_MF + 768], op=ALU.mult), ["PC", "cst"], [T("scF")])
            A("dve", lambda e: e.tensor_tensor(out=scB, in0=PD_[:, :], in1=cst[:, C_MF + 256:C_MF + 768], op=ALU.mult), ["PD", "cst"], [T("scB")])
            A("pe", lambda e: e.matmul(PA_[:, 0:256], lhsT=cst[:, C_LSN:C_LSN + 128], rhs=lap[:, 0:256], start=True, stop=True),
              ["cst", "lap4_%d" % sub], ["PA"])
            A("act", lambda e: e.activation(out=Eend, in_=PA_[:, 0:256], func=AF.Exp), ["PA"], ["lap4_%d" % sub])
            A("dve", lambda e: e.tensor_tensor(out=kte, in0=ktm[:, sub, :], in1=Eend, op=ALU.mult), ["ktm%d" % sub, "lap4_%d" % sub], [T("kdec")])
            for h in range(4):
                p, r = h // 2, h % 2
                osl = slice(h * 128, (h + 1) * 128)
                scr = scF if r == 0 else scB
                A("pe", lambda e, p=p, scr=scr, osl=osl: e.matmul(PE_[:, osl], lhsT=vb[:, sub, osl], rhs=scr[:, p * 128:(p + 1) * 128],
                                                                  start=True, stop=False), ["vb%d" % sub, T("scF"), T("scB")], ["PE"])
                A("pe", lambda e, p=p, scr=scr, osl=osl: e.matmul(PE_[:, osl], lhsT=vb[:, sub, osl], rhs=scr[:, (2 + p) * 128:(3 + p) * 128],
                                                                  start=False, stop=False), ["vb%d" % sub, T("scF"), T("scB")], ["PE"])
                A("pe", lambda e, p=p, r=r, osl=osl: e.matmul(PE_[:, osl], lhsT=Sfb[r * 64:(r + 1) * 64, p, :],
                                                              rhs=qdec[r * 64:(r + 1) * 64, 0, p * 128:(p + 1) * 128], start=False, stop=False),
                  ["Sfb", T("qdec")], ["PE"])
                A("pe", lambda e, p=p, r=r, osl=osl: e.matmul(PE_[:, osl], lhsT=Sbst[r * 64:(r + 1) * 64, j, p, :],
                                                              rhs=qdec[r * 64:(r + 1) * 64, 1, p * 128:(p + 1) * 128], start=False, stop=True),
                  ["Sbst%d" % j, T("qdec")], ["PE"])
            A("dve", lambda e: e.tensor_copy(out=oTs, in_=PE_[:, :]), ["PE"], [T("oTs")])
            A("act", lambda e: e.activation(out=osq, in_=oTs, func=AF.Square), [T("oTs")], [T("scF")])
            A("pe", lambda e: e.matmul(PE_[:, :], lhsT=oneb[:], rhs=osq, start=True, stop=True), ["oneb", T("scF")], ["PE"])
            A("act", lambda e: e.activation(out=rsb, in_=PE_[:, :], func=AF.Ln, scale=1.0 / 128, bias=epsc[:, 0:1]), ["PE", "epsc"], [T("Ei")])
            for p in range(2):
                A("pe", lambda e, p=p: e.matmul(PE_[:, p * 256:(p + 1) * 256], lhsT=kte[:, p * 128:(p + 1) * 128],
                                                rhs=vb[:, sub, p * 256:(p + 1) * 256], start=True, stop=True), [T("kdec"), "vb%d" % sub], ["PE"])
            A("act", lambda e: e.activation(out=rsb, in_=rsb, func=AF.Exp, scale=-0.5), [T("Ei")], [T("Ei")])
            A("dve", lambda e: e.tensor_copy(out=decs[q], in_=Ee[:, 127:256:128]), [T("Ee")], ["decs%d" % q])
            state_update(Sf, "Sf", PE_, "PE", decs[q], "decs%d" % q, add=A)
            A("act", lambda e: e.activation(out=Sfb, in_=Sf, func=AF.Copy), ["Sf"], ["Sfb"])
            A("dve", lambda e: e.tensor_tensor(out=oTs, in0=oTs, in1=rsb, op=ALU.mult), [T("oTs"), T("Ei")], [T("oTs")])
            for h in range(4):
                A("dve", lambda e, h=h: e.scalar_tensor_tensor(
                    out=yT[:, h, tsl], in0=oTs[:, h * 128:(h + 1) * 128], scalar=pv[:, 16 + h:17 + h],
                    in1=sgT[:, h, tsl], op0=ALU.mult, op1=ALU.mult), [T("oTs"), "pv", "sgT"], ["yTa%d" % sub])

        def outproj_sub(i, sub, A):
            j = i * 4 + sub
            tsl = slice(sub * 128, (sub + 1) * 128)
            xsl = xrc[0] % 2
            xrc[0] += 1
            dma(xr[xsl][:], x_own[j * 128:(j + 1) * 128, :], [], ["xr%d" % xsl], "dxr%d" % xsl, add=A)
            for n, (bank, btok) in enumerate(((PF_, "PF"), (PG_, "PG"))):
                for k in range(8):
                    A("pe", lambda e, n=n, k=k, bank=bank: e.matmul(bank[:, :], lhsT=yT[:, k, tsl], rhs=woutb[:, k, n * 512:(n + 1) * 512],
                                                                   start=(k == 0), stop=(k == 7)), ["yTa%d" % sub, "yTb%d" % sub, "woutb"], [btok])
                A("dve", lambda e, n=n, bank=bank: e.tensor_tensor(out=xr[xsl][:, n * 512:(n + 1) * 512], in0=bank[:, :],
                                                                   in1=xr[xsl][:, n * 512:(n + 1) * 512], op=ALU.add),
                  [btok, "xr%d" % xsl], ["xr%d" % xsl])
            dma(x1s[j * 128:(j + 1) * 128, :], xr[xsl][:], ["xr%d" % xsl], ["x1s_%d" % j], "dx1_%d" % xsl, add=A, eng="pool")

        if stages >= 2.5:
            m0(0)
            for i_ in range(NST):
                m1(i_)

        btag = barrier()
        dma(gFb, gF.partition_broadcast(128), [btag], ["gFb"], "dconst2")
        def gu_chunk(f, dst3, src2, tok, use_act):
            sl = wl[0] % 3
            wl[0] += 1
            dma(stg[sl][:], src2[f * 128:(f + 1) * 128, :], [btag], ["stg%d" % sl], "dstg%dpool" % sl, eng="pool")
            o3 = dst3[:, :, f * 128:(f + 1) * 128]
            i3 = stg[sl][:].rearrange("p (k c) -> p k c", k=8)
            if use_act:
                S.add("act", lambda e: e.activation(out=o3, in_=i3, func=AF.Copy), ["stg%d" % sl, btag], ["%s%d" % (tok, f)])
            else:
                S.add("dve", lambda e: e.tensor_copy(out=o3, in_=i3), ["stg%d" % sl, btag], ["%s%d" % (tok, f)])

        def dn_chunk(f):
            sl = wl[0] % 3
            wl[0] += 1
            dma(stg[sl][:], w_down[f * 128:(f + 1) * 128, :], [btag], ["stg%d" % sl], "dstg%dpool" % sl, eng="pool")
            if f % 2 == 0:
                S.add("act", lambda e: e.activation(out=wdnb[:, f, :], in_=stg[sl][:], func=AF.Copy), ["stg%d" % sl, btag], ["wdn%d" % f])
            else:
                S.add("dve", lambda e: e.tensor_copy(out=wdnb[:, f, :], in_=stg[sl][:]), ["stg%d" % sl, btag], ["wdn%d" % f])

        fsl = {}

        def f0(i):
            for sub in range(4):
                j = i * 4 + sub
                fsl[j] = prep_load(x1s[j * 128:(j + 1) * 128, :], ["x1s_%d" % j])
                prep_transpose(fsl[j], xT[:, :, sub * 128:(sub + 1) * 128], "xT", gcol=8)

        oc = [0]

        def f1(i):
            gb = [(PA_, "PA"), (PB_, "PB")]
            ub = [(PC_, "PC"), (PD_, "PD")]
            for f in range(22):
                gbank, gtok = gb[f % 2]
                ubank, utok = ub[f % 2]
                if i == 0:
                    gu_chunk(f, wgb, w_gate, "wg", True)
                    gu_chunk(f, wub, w_up, "wu", False)
                    if f >= 11:
                        dn_chunk(2 * (f - 11))
                        dn_chunk(2 * (f - 11) + 1)
                for k in range(8):
                    S.add("pe", lambda e, f=f, k=k, gbank=gbank: e.matmul(gbank[:, :], lhsT=wgb[:, k, f * 128:(f + 1) * 128], rhs=xT[:, k, :],
                                                                         start=(k == 0), stop=(k == 7)), ["wg%d" % f, "xT"], [gtok])
                for k in range(8):
                    S.add("pe", lambda e, f=f, k=k, ubank=ubank: e.matmul(ubank[:, :], lhsT=wub[:, k, f * 128:(f + 1) * 128], rhs=xT[:, k, :],
                                                                         start=(k == 0), stop=(k == 7)), ["wu%d" % f, "xT"], [utok])
                S.add("act", lambda e, f=f, gbank=gbank: e.activation(out=sgs[f % 2], in_=gbank[:, :], func=AF.Silu), [gtok], ["sgs%d" % (f % 2)])
                S.add("dve", lambda e, f=f, ubank=ubank: e.tensor_tensor(out=actT[:, f, :], in0=ubank[:, :], in1=sgs[f % 2], op=ALU.mult),
                      [utok, "sgs%d" % (f % 2)], ["actT"])
            nxt = i + 1 < NST
            nsl = {}

            def nprep_load(sub):
                jn = (i + 1) * 4 + sub
                nsl[sub] = prep_load(x1s[jn * 128:(jn + 1) * 128, :], ["x1s_%d" % jn])

            if nxt:
                nprep_load(0)
                nprep_load(1)
            for sub in range(4):
                j = i * 4 + sub
                tsl = slice(sub * 128, (sub + 1) * 128)
                xsl = xrc[0] % 2
                xrc[0] += 1
                dma(xr[xsl][:], x1s[j * 128:(j + 1) * 128, :], ["x1s_%d" % j], ["xr%d" % xsl], "dxr%d" % xsl)
                dbanks = ((PE_, "PE"), (PF_, "PF")) if sub % 2 == 0 else ((PG_, "PG"), (PA_, "PA"))
                for n, (bank, btok) in enumerate(dbanks):
                    for f in range(22):
                        S.add("pe", lambda e, n=n, f=f, bank=bank, tsl=tsl: e.matmul(bank[:, :], lhsT=actT[:, f, tsl], rhs=wdnb[:, f, n * 512:(n + 1) * 512],
                                                                                    start=(f == 0), stop=(f == 21)), ["actT", "wdn%d" % f], [btok])
                    S.add("dve", lambda e, n=n, bank=bank, xsl=xsl: e.tensor_tensor(out=xr[xsl][:, n * 512:(n + 1) * 512], in0=bank[:, :],
                                                                                   in1=xr[xsl][:, n * 512:(n + 1) * 512], op=ALU.add),
                          [btok, "xr%d" % xsl], ["xr%d" % xsl])
                if nxt:
                    prep_transpose(nsl[sub], xT[:, :, sub * 128:(sub + 1) * 128], "xT", gcol=8)
                    if sub + 2 < 4:
                        nprep_load(sub + 2)
                S.add("act", lambda e, xsl=xsl: e.activation(out=junk[:], in_=xr[xsl][:], func=AF.Square, accum_out=mstB[:, 0:1]),
                      ["xr%d" % xsl], ["junk", "mstB"])
                S.add("act", lambda e: e.activation(out=mstB[:, 1:2], in_=mstB[:, 0:1], func=AF.Ln, scale=1.0 / D, bias=epsc[:, 0:1]),
                      ["mstB", "epsc"], ["mstB"])
                S.add("act", lambda e: e.activation(out=mstB[:, 2:3], in_=mstB[:, 1:2], func=AF.Exp, scale=-0.5), ["mstB"], ["mstB"])
                osl = oc[0] % 2
                oc[0] += 1
                S.add("dve", lambda e, osl=osl, xsl=xsl: e.scalar_tensor_tensor(out=stg[osl][:], in0=xr[xsl][:], scalar=mstB[:, 2:3], in1=gFb,
                                                                               op0=ALU.mult, op1=ALU.mult),
                      ["xr%d" % xsl, "mstB", "gFb"], ["stg%d" % osl])
                dma(out[j * 128:(j + 1) * 128, :], stg[osl][:], ["stg%d" % osl], [], "dout%d" % osl, eng="pool")

        if stages >= 5:
            f0(0)
            for i_ in range(NST):
                f1(i_)

        S.assign()
        dh = {n: es.enter_context(nc.semaphore("d_" + n)) for n in sorted(dnames)}
        with nc.Block() as block:
            @block.sync
            def _(eng):
                S.emit_engine("sp", eng, sems, dh, final_waits=[n for n in sorted(dnames)])

            @block.scalar
            def _(eng):
                S.emit_engine("act", eng, sems, dh)

            @block.vector
            def _(eng):
                S.emit_engine("dve", eng, sems, dh)

            @block.gpsimd
            def _(eng):
                S.emit_engine("pool", eng, sems, dh)

            @block.tensor
            def _(eng):
                S.emit_engine("pe", eng, sems, dh)
    return nc


def _consts():
    c = np.zeros((128, NCONST), np.float32)
    s = np.arange(128)[:, None]
    t = np.arange(128)[None, :]
    c[:, C_ID:C_ID + 128] = np.eye(128, dtype=np.float32)
    c[:, C_ONE:C_ONE + 128] = 1.0
    n16 = np.float32(-1.0 / 16.0)
    c[:, C_UN:C_UN + 128] = np.where(s <= t, n16, 0)
    c[:, C_UTN:C_UTN + 128] = np.where(s >= t, n16, 0)
    c[:, C_LSN:C_LSN + 128] = np.where(s > t, n16, 0)
    c[:, C_USN:C_USN + 128] = np.where(s < t, n16, 0)
    mf = np.where(s <= t, 1.0, 0.0).astype(np.float32)
    mb = np.where(s >= t, 1.0, 0.0).astype(np.float32)
    c[:, C_MF:C_MF + 512] = np.tile(mf, (1, 4))
    c[:, C_MB:C_MB + 512] = np.tile(mb, (1, 4))
    return c


def _fmajor(w):
    return np.ascontiguousarray(w.reshape(8, 128, 22, 128).transpose(2, 1, 0, 3).reshape(22 * 128, 8 * 128))


def _core_inputs(inp, b, half, T):
    f32 = lambda a: np.ascontiguousarray(np.asarray(a, dtype=np.float32))
    x = np.asarray(inp["x"], dtype=np.float32)
    w_in = np.asarray(inp["w_in"], dtype=np.float32)[0]
    q, k, v, g = w_in[:, 0:256], w_in[:, 256:512], w_in[:, 512:1024], w_in[:, 1024:1536]
    lrf, lrb = w_in[:, 1536:1552], w_in[:, 1552:1568]
    u, gvv = w_in[:, 1568:2080], w_in[:, 2080:2592]
    wdf, bdf = np.asarray(inp["w_decay_f"], np.float32)[0], np.asarray(inp["b_decay_f"], np.float32)[0]
    wdb, bdb = np.asarray(inp["w_decay_b"], np.float32)[0], np.asarray(inp["b_decay_b"], np.float32)[0]
    ws = np.asarray(inp["w_spatial"], np.float32)[0]
    bs = np.asarray(inp["b_spatial"], np.float32)[0]
    if half == 1:
        x_own, x_oth = x[b, T:2 * T], x[b, 0:T]
        LF, LB, WF, BF_, WB, BB = lrf, lrb, wdf, bdf, wdb, bdb
    else:
        x_own, x_oth = x[b, 0:T][::-1], x[b, T:2 * T][::-1]
        LF, LB, WF, BF_, WB, BB = lrb, lrf, wdb, bdb, wdf, bdf
        ws = ws[:, ::-1, ::-1]
        bs = bs[:, ::-1]
    w_in_dev = np.concatenate([q, k, g, u, k, LF, LB, v, gvv], axis=1)
    assert w_in_dev.shape == (D, NIN)
    wdblk = np.zeros((33, 512), np.float32)
    wdblk[0:16, 0:256] = WF
    wdblk[16:32, 256:512] = WB
    wdblk[32, 0:256] = BF_
    wdblk[32, 256:512] = BB
    pvec = np.zeros((128, 24), np.float32)
    pvec[:, 0:8] = np.asarray(inp["norm1_g"], np.float32)[0].reshape(8, 128).T
    pvec[:, 8:16] = np.asarray(inp["norm2_g"], np.float32)[0].reshape(8, 128).T
    pvec[:, 16:20] = np.asarray(inp["gla_norm_g"], np.float32)[0].reshape(4, 128).T
    pvec[:, 20:24] = np.asarray(inp["gmlp_ln_g"], np.float32)[0].reshape(4, 128).T
    rvec = np.zeros((1, 1024), np.float32)
    rvec[0, 0:512] = np.asarray(inp["gmlp_ln_b"], np.float32)[0]
    rvec[0, 512:1024] = bs.reshape(512)
    wsT = np.transpose(ws, (2, 0, 1)).reshape(128, 512)
    return {
        "x_own": f32(x_own), "x_oth": f32(x_oth), "w_in": f32(w_in_dev), "wdblk": wdblk,
        "consts": _consts(), "pvec": pvec, "rvec": rvec, "gF": f32(inp["final_norm_g"]),
        "wsT": f32(wsT), "w_out": f32(np.asarray(inp["w_out"], np.float32)[0]),
        "w_gate": f32(_fmajor(np.asarray(inp["w_gate"], np.float32)[0])), "w_up": f32(_fmajor(np.asarray(inp["w_up"], np.float32)[0])),
        "w_down": f32(np.asarray(inp["w_down"], np.float32)[0]),
    }


def kernel(**inputs):
    x = np.asarray(inputs["x"])
    B, SEQ, _ = x.shape
    T = SEQ // 2
    NT = T // 128
    ncores = 2 * B
    nc = build_nc(NT)
    in_maps = [_core_inputs(inputs, c // 2, c % 2, T) for c in range(ncores)]
    res = run_bass_kernel_spmd(nc, in_maps, core_ids=list(range(ncores)))
    outp = np.empty((B, SEQ, D), np.float32)
    for c in range(ncores):
        o = np.asarray(res.results[c]["out"], dtype=np.float32)
        b, half = c // 2, c % 2
        if half == 1:
            outp[b, T:2 * T] = o
        else:
            outp[b, 0:T] = o[::-1]
    return outp
```
